# Optimizing a Trainium2 kernel written in Bass

```python
import math
import jax, jax.numpy as jnp
from jax import lax
import numpy as np


D_MODEL = 1024
BATCH = 8
SEQ = 2048
DEPTH = 2

CHUNK = 64
EPS = 1e-6

GM_WIDTH = 1024
GM_GROUPS = 4
GM_BLOCK = 128

MLA_HEADS = 8
MLA_NOPE = 128
MLA_ROPE = 64
MLA_VDIM = 128
MLA_QK_DIM = MLA_NOPE + MLA_ROPE
MLA_Q_RANK = 384
MLA_KV_RANK = 256
MLA_WIDTH = MLA_HEADS * MLA_VDIM
ROPE_THETA = 10000.0
Q_BLOCK = 128

LRU_WIDTH = 1280
LRU_BLOCKS = 16
LRU_BW = LRU_WIDTH // LRU_BLOCKS
LRU_C = 8.0
CONV_W = 4

IN_SIZES = (GM_WIDTH, GM_WIDTH, GM_WIDTH, MLA_Q_RANK, MLA_KV_RANK, MLA_ROPE, MLA_WIDTH,
            LRU_WIDTH, LRU_WIDTH, D_MODEL, D_MODEL, D_MODEL)
N_IN = sum(IN_SIZES)

kernel_name = "hybrid_gmlp_mla_rglru_chunk_causal"


def _split_points():
    return [int(s) for s in np.cumsum(IN_SIZES)[:-1]]


def rmsnorm(x, g):
    xf = x.astype(jnp.float32)
    y = xf * lax.rsqrt(jnp.mean(xf * xf, axis=-1, keepdims=True) + EPS)
    return (y * g.astype(jnp.float32)).astype(x.dtype)


def layernorm(x, g, b):
    xf = x.astype(jnp.float32)
    mu = jnp.mean(xf, axis=-1, keepdims=True)
    var = jnp.mean(jnp.square(xf - mu), axis=-1, keepdims=True)
    y = (xf - mu) * lax.rsqrt(var + EPS)
    return (y * g.astype(jnp.float32) + b.astype(jnp.float32)).astype(x.dtype)


def rope(t, cos, sin):
    half = t.shape[-1] // 2
    t1, t2 = t[..., :half], t[..., half:]
    return jnp.concatenate([t1 * cos - t2 * sin, t2 * cos + t1 * sin], axis=-1)


def gmlp_spatial(u, v, ln_g, ln_b, ws, bs):
    B, S, _ = v.shape
    v = layernorm(v, ln_g, ln_b)
    vb = v.reshape(B, S // GM_BLOCK, GM_BLOCK, GM_GROUPS, GM_WIDTH // GM_GROUPS)
    idx = jnp.arange(GM_BLOCK)
    mask = (idx[None, :] // CHUNK) <= (idx[:, None] // CHUNK)
    ws_m = jnp.where(mask[None], ws, jnp.zeros_like(ws))
    sv = jnp.einsum('gij,bnjgc->bnigc', ws_m, vb) + bs.T[None, None, :, :, None]
    return u * sv.reshape(B, S, GM_WIDTH)


def mla_attention(c_q, c_kv, k_rope_in, q_norm_g, w_uq, kv_norm_g, w_ukv):
    B, S, _ = c_q.shape
    H = MLA_HEADS
    q = (rmsnorm(c_q, q_norm_g) @ w_uq).reshape(B, S, H, MLA_QK_DIM)
    kv = (rmsnorm(c_kv, kv_norm_g) @ w_ukv).reshape(B, S, H, MLA_NOPE + MLA_VDIM)
    q_nope, q_rope = q[..., :MLA_NOPE], q[..., MLA_NOPE:]
    k_nope, v = kv[..., :MLA_NOPE], kv[..., MLA_NOPE:]

    pos = jnp.arange(S, dtype=jnp.float32)
    inv_freq = ROPE_THETA ** (-jnp.arange(0, MLA_ROPE, 2, dtype=jnp.float32) / MLA_ROPE)
    ang = pos[:, None] * inv_freq[None, :]
    cos = jnp.cos(ang).astype(q.dtype)
    sin = jnp.sin(ang).astype(q.dtype)
    q_rope = rope(q_rope, cos[None, :, None, :], sin[None, :, None, :])
    k_rope = rope(k_rope_in, cos[None], sin[None])

    q = jnp.concatenate([q_nope, q_rope], axis=-1)
    k = jnp.concatenate([k_nope, jnp.broadcast_to(k_rope[:, :, None, :], (B, S, H, MLA_ROPE))], axis=-1)
    scale = 1.0 / math.sqrt(MLA_QK_DIM)
    nb = S // Q_BLOCK
    qb = q.reshape(B, nb, Q_BLOCK, H, MLA_QK_DIM).transpose(1, 0, 2, 3, 4)
    key_chunk = jnp.arange(S) // CHUNK

    def one_block(args):
        qi, bi = args
        s = jnp.einsum('bqhd,bkhd->bhqk', qi, k).astype(jnp.float32) * scale
        q_chunk = (bi * Q_BLOCK + jnp.arange(Q_BLOCK)) // CHUNK
        mask = key_chunk[None, :] <= q_chunk[:, None]
        s = jnp.where(mask[None, None], s, -1e30)
        p = jax.nn.softmax(s, axis=-1).astype(v.dtype)
        return jnp.einsum('bhqk,bkhd->bqhd', p, v)

    o = lax.map(one_block, (qb, jnp.arange(nb)))
    return o.transpose(1, 0, 2, 3, 4).reshape(B, S, MLA_WIDTH)


def rg_lru(x_c, conv_w, conv_b, w_a, b_a, w_x, b_x, lam):
    B, S, C = x_c.shape
    xc = lax.conv_general_dilated(
        x_c, conv_w[:, None, :].astype(x_c.dtype), window_strides=(1,), padding=[(CONV_W - 1, 0)],
        dimension_numbers=('NWC', 'WIO', 'NWC'), feature_group_count=C) + conv_b
    xb = xc.reshape(B, S, LRU_BLOCKS, LRU_BW)
    r = jax.nn.sigmoid(jnp.einsum('bshi,hij->bshj', xb, w_a).reshape(B, S, C) + b_a)
    i = jax.nn.sigmoid(jnp.einsum('bshi,hij->bshj', xb, w_x).reshape(B, S, C) + b_x)
    rf = r.astype(jnp.float32)
    log_a = -LRU_C * rf * jax.nn.softplus(-lam.astype(jnp.float32))
    a = jnp.exp(log_a)
    mult = jnp.sqrt(jnp.maximum(1.0 - jnp.exp(2.0 * log_a), 0.0))
    bterm = mult * (i.astype(jnp.float32) * xc.astype(jnp.float32))

    def combine(e1, e2):
        a1, b1 = e1
        a2, b2 = e2
        return a1 * a2, a2 * b1 + b2

    _, h = lax.associative_scan(combine, (a, bterm), axis=1)
    return h.astype(x_c.dtype)


def setup_inputs(seed: int = 0) -> dict:
    key = jax.random.key(seed)
    ks = iter(jax.random.split(key, 32))
    L, D = DEPTH, D_MODEL

    def nrm(shape, scale):
        return jax.random.normal(next(ks), shape, jnp.float32) * scale

    def gain(shape):
        return 1.0 + nrm(shape, 0.02)

    a_init = jax.random.uniform(next(ks), (L, LRU_WIDTH), jnp.float32, 0.9, 0.999)
    s = a_init ** (1.0 / LRU_C)
    lam = jnp.log(s) - jnp.log1p(-s)

    return {
        "x": nrm((BATCH, SEQ, D), 1.0),
        "pre_norm_g": gain((L, D)),
        "w_in": nrm((L, D, N_IN), D ** -0.5),
        "gm_ln_g": gain((L, GM_WIDTH)),
        "gm_ln_b": nrm((L, GM_WIDTH), 0.02),
        "gm_ws": nrm((L, GM_GROUPS, GM_BLOCK, GM_BLOCK), GM_BLOCK ** -0.5),
        "gm_bs": 1.0 + nrm((L, GM_GROUPS, GM_BLOCK), 0.02),
        "mla_q_norm_g": gain((L, MLA_Q_RANK)),
        "mla_w_uq": nrm((L, MLA_Q_RANK, MLA_HEADS * MLA_QK_DIM), MLA_Q_RANK ** -0.5),
        "mla_kv_norm_g": gain((L, MLA_KV_RANK)),
        "mla_w_ukv": nrm((L, MLA_KV_RANK, MLA_HEADS * (MLA_NOPE + MLA_VDIM)), MLA_KV_RANK ** -0.5),
        "lru_conv_w": nrm((L, CONV_W, LRU_WIDTH), CONV_W ** -0.5),
        "lru_conv_b": nrm((L, LRU_WIDTH), 0.01),
        "lru_w_a": nrm((L, LRU_BLOCKS, LRU_BW, LRU_BW), LRU_BW ** -0.5),
        "lru_b_a": nrm((L, LRU_WIDTH), 0.01),
        "lru_w_x": nrm((L, LRU_BLOCKS, LRU_BW, LRU_BW), LRU_BW ** -0.5),
        "lru_b_x": nrm((L, LRU_WIDTH), 0.01),
        "lru_lambda": lam,
        "w_proj_a": nrm((L, GM_WIDTH, D), GM_WIDTH ** -0.5),
        "w_proj_b": nrm((L, MLA_WIDTH, D), MLA_WIDTH ** -0.5),
        "w_proj_c": nrm((L, LRU_WIDTH, D), LRU_WIDTH ** -0.5),
        "w_out": nrm((L, D, D), D ** -0.5),
        "post_norm_g": gain((L, D)),
    }


def reference(x, pre_norm_g, w_in, gm_ln_g, gm_ln_b, gm_ws, gm_bs, mla_q_norm_g, mla_w_uq,
              mla_kv_norm_g, mla_w_ukv, lru_conv_w, lru_conv_b, lru_w_a, lru_b_a, lru_w_x,
              lru_b_x, lru_lambda, w_proj_a, w_proj_b, w_proj_c, w_out, post_norm_g):
    cuts = _split_points()
    for l in range(DEPTH):
        h = rmsnorm(x, pre_norm_g[l])
        proj = h @ w_in[l]
        (u, v, z_a, c_q, c_kv, k_rope, z_b, x_c, z_c,
         g_a, g_b, g_c) = jnp.split(proj, cuts, axis=-1)

        y_a = gmlp_spatial(u, v, gm_ln_g[l], gm_ln_b[l], gm_ws[l], gm_bs[l]) * jax.nn.silu(z_a)
        y_b = mla_attention(c_q, c_kv, k_rope, mla_q_norm_g[l], mla_w_uq[l],
                            mla_kv_norm_g[l], mla_w_ukv[l]) * jax.nn.silu(z_b)
        y_c = rg_lru(x_c, lru_conv_w[l], lru_conv_b[l], lru_w_a[l], lru_b_a[l],
                     lru_w_x[l], lru_b_x[l], lru_lambda[l]) * jax.nn.silu(z_c)

        merged = (jax.nn.sigmoid(g_a) * (y_a @ w_proj_a[l])
                  + jax.nn.sigmoid(g_b) * (y_b @ w_proj_b[l])
                  + jax.nn.sigmoid(g_c) * (y_c @ w_proj_c[l]))
        x = x + rmsnorm(merged @ w_out[l], post_norm_g[l])
    return x
```

```python
import math
from contextlib import ExitStack

import numpy as np
import concourse.bass as bass
import concourse.mybir as mybir
from concourse.bass_utils import run_bass_kernel_spmd

F32 = mybir.dt.float32
BF16 = mybir.dt.bfloat16
ALU = mybir.AluOpType
AF = mybir.ActivationFunctionType

D = 1024
SEQ = 2048
NT = 512
NCH = SEQ // NT
TPC = NT // 128
EPS = 1e-6
H = 8
N_IN = 10432
O_U, O_V, O_ZA, O_CQ, O_CKV, O_KR, O_ZB, O_XC, O_ZC, O_GA, O_GB, O_GC = (
    0, 1024, 2048, 3072, 3456, 3712, 3776, 4800, 6080, 7360, 8384, 9408)
SCALE = 1.0 / math.sqrt(192.0)
V_LNG, V_LNB, V_QG, V_KVG, V_CW, V_CB, V_BA, V_BX, V_LAM = 0, 8, 16, 19, 21, 61, 71, 81, 91
NV = 101
GATE_PAIRS = [(0, 0), (0, 1), (1, 0), (1, 1), (1, 2), (2, 1), (2, 2), (2, 3), (3, 2), (3, 3), (3, 4),
              (4, 3), (4, 4)]
WSLOT = 4096
NSLOT = 4
LOOKAHEAD = 2
USE_SCRATCH = True
NTB = 12
NB16 = 8


def req_plan():
    def cols(arr, c0, c1):
        return ("cols", arr, c0, c1)
    P = []
    P.append(("cq", [(0, [8, 384], cols("w_in", O_CQ, O_CQ + 384))]))
    P.append(("ckv", [(0, [8, 320], cols("w_in", O_CKV, O_CKV + 320)), (8 * 320, [8, 64], ("ropesw", "w_in", O_KR))]))
    for n in range(2):
        P.append(("wv%d" % n, [(0, [8, 512], cols("w_in", O_V + n * 512, O_V + (n + 1) * 512))]))
    for h in range(H):
        q0c = h * 192
        P.append(("h%d" % h, [
            (0, [3, 192], cols("w_uq", q0c, q0c + 192)),
            (576, [3, 64], ("ropesw", "w_uq", q0c + 128)),
            (768, [2, 256], cols("w_ukv", h * 256, (h + 1) * 256)),
            (1280, [8, 128], cols("w_in", O_ZB + h * 128, O_ZB + (h + 1) * 128))]))
    for n in range(2):
        P.append(("u%d" % n, [(0, [8, 512], cols("w_in", O_U + n * 512, O_U + (n + 1) * 512))]))
        P.append(("z%d" % n, [(0, [8, 512], cols("w_in", O_ZA + n * 512, O_ZA + (n + 1) * 512))]))
    for hf in range(2):
        P.append(("xc%da" % hf, [(0, [8, 384], cols("w_in", O_XC + hf * 640, O_XC + hf * 640 + 384))]))
        P.append(("xc%db" % hf, [(0, [8, 256], cols("w_in", O_XC + hf * 640 + 384, O_XC + (hf + 1) * 640))]))
        P.append(("g%d" % hf, [(0, [13, 128], ("pad", "wa_pad", hf)), (13 * 128, [13, 128], ("pad", "wx_pad", hf))]))
        P.append(("zc%da" % hf, [(0, [8, 384], cols("w_in", O_ZC + hf * 640, O_ZC + hf * 640 + 384))]))
        P.append(("zc%db" % hf, [(0, [8, 256], cols("w_in", O_ZC + hf * 640 + 384, O_ZC + (hf + 1) * 640))]))
    for dc in range(8):
        P.append(("G%d" % dc, [(m_ * 1024, [8, 128], cols("w_in", O_GA + m_ * 1024 + dc * 128,
                                                            O_GA + m_ * 1024 + (dc + 1) * 128)) for m_ in range(3)]))
        P.append(("P%d" % dc, [(0, [8, 128], cols("w_pa", dc * 128, (dc + 1) * 128)),
                               (1024, [8, 128], cols("w_pb", dc * 128, (dc + 1) * 128)),
                               (2048, [10, 128], cols("w_pc", dc * 128, (dc + 1) * 128))]))
    for n in range(2):
        P.append(("wo%d" % n, [(0, [8, 512], cols("w_out", n * 512, (n + 1) * 512))]))
    return P


def req_used(parts):
    return max(off + int(np.prod(shp)) for off, shp, _ in parts)


def build_slot_images(arrs):
    plan = req_plan()
    L = 2
    img = np.zeros((L, len(plan), 128, WSLOT), np.float32)

    def kp(a):
        k = a.shape[0] // 128
        return a.reshape(k, 128, a.shape[1]).transpose(1, 0, 2).reshape(128, -1)

    for l in range(L):
        for j, (name, parts) in enumerate(plan):
            for off, shp, spec in parts:
                n = int(np.prod(shp))
                if spec[0] == "cols":
                    data = kp(arrs[spec[1]][l][:, spec[2]:spec[3]])
                elif spec[0] == "ropesw":
                    a = arrs[spec[1]][l]
                    c0 = spec[2]
                    data = kp(np.concatenate([a[:, c0 + 32:c0 + 64], a[:, c0:c0 + 32]], axis=1))
                else:
                    a = arrs[spec[1]][l, spec[2] * 13:(spec[2] + 1) * 13]
                    data = a.transpose(1, 0, 2).reshape(128, -1)
                assert data.shape == (128, n), (name, data.shape, n)
                img[l, j, :, off:off + n] = data
    return img


class Sched:
    ENGS = ("pe", "act", "dve", "pool", "sp")

    def __init__(self):
        self.ops = {e: [] for e in self.ENGS}
        self.last_w = {}
        self.readers = {}
        self.dma_cnt = {}
        self.epoch = 0
        self.tag = 'setup'

    def _deps(self, reads, writes):
        d = {}

        def put(t):
            k, v = t
            if d.get(k, -1) < v:
                d[k] = v

        for k in reads:
            if k in self.last_w:
                put(self.last_w[k])
        for k in writes:
            if k in self.last_w:
                put(self.last_w[k])
            for kk, vv in self.readers.get(k, {}).items():
                put((kk, vv))
        return d

    def _commit(self, tok, reads, writes):
        k, v = tok
        for b in reads:
            r = self.readers.setdefault(b, {})
            if r.get(k, -1) < v:
                r[k] = v
        for b in writes:
            self.last_w[b] = tok
            self.readers[b] = {}

    def add(self, eng, fn, r=(), w=()):
        pr = [k for k in r if isinstance(k, tuple) and k and k[0] == "ps"]
        if pr:
            r = [k for k in r if k not in pr]
            w = list(w) + pr
        deps = self._deps(r, w)
        if eng == "pe":
            deps.pop(("e", "pe"), None)
        idx = len(self.ops[eng])
        self.ops[eng].append(dict(fn=fn, deps=deps, epoch=self.epoch, dma=None, tag=self.tag))
        self._commit((("e", eng), idx), r, w)

    def dma(self, q, fn, r=(), w=(), key=None, first=True):
        deps = self._deps(r, w)
        if not first:
            deps.pop(("d", key), None)
        n = self.dma_cnt.get(key, 0) + 1
        self.dma_cnt[key] = n
        self.ops[q].append(dict(fn=fn, deps=deps, epoch=self.epoch, dma=key, tag=self.tag))
        self._commit((("d", key), n * 16), r, w)

    def n_epochs(self):
        return self.epoch + 1

    def emit(self, eng, handle, eng_sems, dma_sems, sig, rank):
        waited = {}
        for idx, op in enumerate(self.ops[eng]):
            for (kind, x), v in op["deps"].items():
                if kind == "e":
                    ep, val = rank[(x, v)]
                    sem = eng_sems[x][ep]
                else:
                    sem, val = dma_sems[x], v
                key = id(sem)
                if waited.get(key, 0) < val:
                    handle.wait_ge(sem, val)
                    waited[key] = val
            ins = op["fn"](handle)
            if op["dma"] is not None:
                ins.then_inc(dma_sems[op["dma"]], 16)
            elif idx in sig[eng]:
                ins.then_inc(eng_sems[eng][op["epoch"]], 1)

    def analyse(self):
        sig = {e: set() for e in self.ENGS}
        for e in self.ENGS:
            for op in self.ops[e]:
                for (kind, x), v in op["deps"].items():
                    if kind == "e":
                        sig[x].add(v)
        rank = {}
        for e in self.ENGS:
            cnt = {}
            for idx, op in enumerate(self.ops[e]):
                if idx in sig[e]:
                    ep = op["epoch"]
                    cnt[ep] = cnt.get(ep, 0) + 1
                    rank[(e, idx)] = (ep, cnt[ep])
        return sig, rank


def build_program(layers, debug=False, nch=NCH, stop=9):
    nc = bass.Bass("TRN2", target_bir_lowering=False)
    NL = len(layers)
    L2 = 2

    def din(name, shape, dt=F32):
        return nc.dram_tensor(name, list(shape), dt, kind="ExternalInput").ap()

    x_d = din("x", [SEQ, D])
    out_d = nc.dram_tensor("out", [SEQ, D], F32, kind="ExternalOutput").ap()
    PLAN = req_plan()
    NREQ = len(PLAN)
    wimg_d = din("wimg", [L2, NREQ, 128, WSLOT])
    wscr_d = nc.dram_tensor("wscr", [L2, NREQ, 128, WSLOT], BF16, kind="Internal").ap()
    wsT_d = din("wsT", [L2, 128, 4, 128])
    bs_d = din("bs", [L2, 512])
    vecs_d = din("vecs", [L2, 128, NV])
    gpre_d = din("gpre", [L2, D])
    gpost_d = din("gpost", [L2, D])
    cos_d = din("cos2", [64, SEQ])
    sin_d = din("sin2", [64, SEQ])
    ident_d = din("ident", [128, 128])

    S = Sched()
    es = ExitStack()

    def sb(name, shape, dt):
        return es.enter_context(nc.sbuf_tensor(name, list(shape), dt))

    with es:
        ps = es.enter_context(nc.psum_tensor("ps", [128, 8, 512], F32))
        x_sb = sb("x_sb", [128, TPC, D], F32)
        ckv_c = [sb(f"ckv{i}", [128, 2, SEQ], BF16) for i in range(NL)]
        kr_c = [sb(f"kr{i}", [128, SEQ], BF16) for i in range(NL)]
        hist = [sb(f"hist{i}", [128, 10, 4], F32) for i in range(NL)]
        hst = [sb(f"hst{i}", [128, 10], F32) for i in range(NL)]
        vecs = [sb(f"vecs{i}", [128, NV], F32) for i in range(NL)]
        nca = [sb(f"nca{i}", [128, 40], F32) for i in range(NL)]
        Bt = [sb(f"Bt{i}", [128, 8, 128], F32) for i in range(NL)]
        wsT_bf = [sb(f"wsTb{i}", [128, 4, 128], BF16) for i in range(NL)]
        ident = sb("ident_sb", [128, 128], BF16)
        ones_f = sb("ones_f", [128, 128], F32)
        ones_b = sb("ones_b", [128, 128], BF16)
        gbc = sb("gbc", [128, D], F32)
        cs_sb = sb("cs_sb", [64, 2, NT], F32)
        ring = [sb(f"ring{i}", [128, WSLOT], BF16) for i in range(NSLOT)]
        hT = sb("hT", [128, 8, NT], BF16)
        T = sb("T", [128, NTB, NT], F32)
        B16 = sb("B16", [128, NB16, NT], BF16)
        htok = sb("htok", [128, 4, D], BF16)
        qbuf = sb("qbuf", [128, 2, 2, NT], BF16)
        small = sb("small", [128, 64], F32)
        LN_QUARTER = sb("lnq", [128, 1], F32)
        bnst = sb("bnst", [128, 2, 6], F32)
        cqn = sb("cqn", [128, 3, NT], BF16)
        k_h = sb("k_h", [128, SEQ], BF16)
        v_h = sb("v_h", [128, 16, 128], BF16)
        y_a = sb("y_a", [128, 8, NT], BF16)
        y_b = sb("y_b", [128, 8, NT], BF16)
        y_c = sb("y_c", [128, 10, NT], BF16)
        xcf = sb("xcf", [128, 5, NT], F32)
        xcb = sb("xcb", [128, 5, NT], BF16)
        raw = sb("raw", [128, 2, NT + 4], F32)

        mrg = htok[:].rearrange("p t (a n) -> p (t a) n", a=2)
        st = dict(ps=0, t=0, b=0, ring=0, sm=0, pslim=8, tlim=NTB)

        def next_ps(n=1):
            b = st["ps"]
            if n == 2 and b % 2 == 1:
                b += 1
            lim = st["pslim"]
            if b + n > lim:
                b = 0
            st["ps"] = (b + n) % lim
            return b

        def psk(b):
            return ("ps", b)

        def next_t(n=1):
            b = st["t"]
            lim = st["tlim"]
            if b + n > lim:
                b = 0
            st["t"] = (b + n) % lim
            return b

        def tk(b):
            return ("T", b)

        def next_b():
            b = st["b"]
            st["b"] = (b + 1) % NB16
            return b

        def bk(b):
            return ("B", b)

        def next_sm(n=1):
            b = st["sm"]
            if b + n > 64:
                b = 0
            st["sm"] = (b + n) % 64
            return b

        wreqs = []
        for c_ in range(nch):
            for l_ in layers:
                for j_, (name_, parts_) in enumerate(PLAN):
                    wreqs.append((name_, parts_, l_, j_))
        wst = dict(cur=0, issued=0, views={})

        def wissue(i):
            name, parts, l_, j_ = wreqs[i]
            s = i % NSLOT
            key = ("w", s)
            views = []
            for off, shp, _spec in parts:
                n = int(np.prod(shp))
                v = ring[s][:, off:off + n]
                if len(shp) == 2:
                    v = v.rearrange("p (a b) -> p a b", a=shp[0], b=shp[1])
                views.append(v)
            used = req_used(parts)
            first_chunk = i < len(layers) * NREQ
            SCR = ("scr", l_, j_)
            if first_chunk or not USE_SCRATCH:
                S.dma("pool", (lambda e, s=s, l_=l_, j_=j_, used=used: e.dma_start(
                    out=ring[s][:, 0:used], in_=wimg_d[l_, j_, :, 0:used], max_dma_last_dim=8192)),
                    r=(), w=(key,), key=key)
                if USE_SCRATCH and nch > 1:
                    S.dma("sp", (lambda e, s=s, l_=l_, j_=j_, used=used: e.dma_start(
                        out=wscr_d[l_, j_, :, 0:used], in_=ring[s][:, 0:used])),
                        r=(key,), w=(SCR,), key=("wst", s))
            else:
                S.dma("sp", (lambda e, s=s, l_=l_, j_=j_, used=used: e.dma_start(
                    out=ring[s][:, 0:used], in_=wscr_d[l_, j_, :, 0:used])),
                    r=(SCR,), w=(key,), key=("wh", s))
            wst["views"][i] = (key, views)

        def wnext(name):
            i = wst["cur"]
            assert wreqs[i][0] == name, (wreqs[i][0], name)
            while wst["issued"] <= min(i + LOOKAHEAD, len(wreqs) - 1):
                wissue(wst["issued"])
                wst["issued"] += 1
            wst["cur"] = i + 1
            return wst["views"].pop(i)

        def mm(out, lhsT, rhs, start, stop, r, w):
            S.add("pe", (lambda e: e.matmul(out, lhsT, rhs, start=start, stop=stop)), r=r, w=w)

        def act(out, in_, func, r, w, bias=None, scale=None, accum=None):
            kw = {}
            if bias is not None:
                kw["bias"] = bias
            if scale is not None:
                kw["scale"] = scale
            if accum is not None:
                kw["accum_out"] = accum
            S.add("act", (lambda e: e.activation(out=out, in_=in_, func=func, **kw)), r=r, w=w)

        def sigmoid3(dst, src, r_src, w_dst, scale=-1.0, bias=None):
            act(dst, src, AF.Exp, r=r_src, w=w_dst, scale=scale, bias=bias)
            act(dst, dst, AF.Ln, r=w_dst, w=w_dst, bias=1.0)
            act(dst, dst, AF.Exp, r=w_dst, w=w_dst, scale=-1.0)

        def tanh_half(dst, src, r_src, w_dst, bias=None):
            act(dst, src, AF.Tanh, r=r_src, w=w_dst, scale=0.5, bias=bias)

        def dve(fn, r, w):
            S.add("dve", fn, r=r, w=w)

        def tt(out, in0, in1, op, r, w, eng="dve"):
            S.add(eng, (lambda e: e.tensor_tensor(out=out, in0=in0, in1=in1, op=op)), r=r, w=w)

        def stt(out, in0, scalar, in1, op0, op1, r, w):
            S.add("dve", (lambda e: e.scalar_tensor_tensor(out=out, in0=in0, scalar=scalar, in1=in1,
                                                           op0=op0, op1=op1)), r=r, w=w)

        def ts(out, in0, s1, s2, op0, op1, r, w, eng="dve"):
            if s2 is None:
                S.add(eng, (lambda e: e.tensor_scalar(out=out, in0=in0, scalar1=s1, scalar2=None, op0=op0)),
                      r=r, w=w)
            else:
                S.add(eng, (lambda e: e.tensor_scalar(out=out, in0=in0, scalar1=s1, scalar2=s2, op0=op0,
                                                      op1=op1)), r=r, w=w)

        def cp(out, in_, r, w, eng):
            if eng == "act":
                S.add("act", (lambda e: e.activation(out=out, in_=in_, func=AF.Copy)), r=r, w=w)
            else:
                S.add(eng, (lambda e: e.tensor_copy(out=out, in_=in_)), r=r, w=w)

        def rsqrt_small(dst, src, r, w, scale, eps=EPS):
            act(dst, src, AF.Ln, r=r, w=w, bias=eps, scale=scale)
            act(dst, dst, AF.Exp, r=w, w=w, scale=-0.5)

        wsT_f = T[:, 0, :].rearrange("p (g i) -> p g i", g=4)
        bs_bc = T[:, 1, :].rearrange("p (g i) -> p g i", g=4)
        S.dma("pool", lambda e: e.dma_start(out=ident[:], in_=ident_d[:, :]), w=("ident",), key="c0")
        S.add("dve", lambda e: e.memset(ones_f[:], 1.0), w=("ones_f",))
        S.add("dve", lambda e: e.memset(ones_b[:], 1.0), w=("ones_b",))
        S.add("dve", lambda e: e.memset(LN_QUARTER[:], -1.3862943611198906), w=("lnq",))
        for qp_ in range(2):
            S.add("dve", (lambda e, qp_=qp_: e.memset(qbuf[64:128, qp_, 1, :], 0.0)), w=(("qr", qp_),))
        for li_ in range(NL):
            S.add("dve", (lambda e, li_=li_: e.memset(kr_c[li_][64:128, :], 0.0)), w=(("kr", li_),))
        for li, l in enumerate(layers):
            S.add("dve", (lambda e, li=li: e.memset(hist[li][:], 0.0)), w=(("hist", li),))
            S.add("dve", (lambda e, li=li: e.memset(hst[li][:], 0.0)), w=(("hst", li),))
            S.dma("sp", (lambda e, li=li, l=l: e.dma_start(out=vecs[li][:], in_=vecs_d[l])), w=(("vecs", li),),
                  key=("c1", li))
            S.dma("sp", (lambda e, l=l: e.dma_start(out=wsT_f, in_=wsT_d[l])), w=(tk(0),), key=("c2", li))
            S.dma("sp", (lambda e, l=l: e.dma_start(
                out=T[:, 1, :], in_=bs_d[l].partition_broadcast(128))),
                w=(tk(1),), key=("c3", li))
            S.add("dve", lambda e: e.memset(wsT_f[64:128, :, 0:64], 0.0), r=(), w=(tk(0),))
            cp(wsT_bf[li][:], wsT_f, r=(tk(0),), w=(("wsT_bf", li),), eng="dve")
            b = next_ps()
            cp(B16[:, 0, :], T[:, 0, :], r=(tk(0),), w=(bk(0),), eng="dve")
            tt(B16[:, 1, :], T[:, 0, :], B16[:, 0, :], ALU.subtract, r=(tk(0), bk(0)), w=(bk(1),))
            mm(ps[:, b, :], ones_b[:], B16[:, 0, :], True, False, r=("ones_b", bk(0)), w=(psk(b),))
            mm(ps[:, b, :], ones_b[:], B16[:, 1, :], False, True, r=("ones_b", bk(1)), w=(psk(b),))
            for cc in range(8):
                g = cc // 2
                stt(Bt[li][:, cc, :], ps[:, b, g * 128:(g + 1) * 128], vecs[li][:, V_LNB + cc:V_LNB + cc + 1],
                    bs_bc[:, g, :], ALU.mult, ALU.add, r=(psk(b), ("vecs", li), tk(1)), w=(("Bt", li),))
            lam = vecs[li][:, V_LAM:V_LAM + 10]
            act(nca[li][:, 0:10], lam, AF.Exp, r=(("vecs", li),), w=(("nca", li),), scale=-1.0)
            act(nca[li][:, 0:10], nca[li][:, 0:10], AF.Ln, r=(("nca", li),), w=(("nca", li),), bias=1.0)
            ts(nca[li][:, 10:20], nca[li][:, 0:10], -8.0, None, ALU.mult, None, r=(("nca", li),),
               w=(("nca", li),))
            ts(nca[li][:, 0:10], nca[li][:, 0:10], -4.0, None, ALU.mult, None, r=(("nca", li),), w=(("nca", li),))
            ts(nca[li][:, 20:30], vecs[li][:, V_BA:V_BA + 10], 0.5, None, ALU.mult, None,
               r=(("vecs", li),), w=(("nca", li),))
            ts(nca[li][:, 30:40], vecs[li][:, V_BX:V_BX + 10], 0.5, None, ALU.mult, None,
               r=(("vecs", li),), w=(("nca", li),))

        x_view = x_d.rearrange("(t p) d -> p t d", p=128)
        out_view = out_d.rearrange("(t p) d -> p t d", p=128)

        for c in range(nch):
            c0 = c * NT
            S.epoch += 1
            if c == 0:
                for t_ in range(TPC):
                    S.dma("sp", (lambda e, t_=t_: e.dma_start(out=x_sb[:, t_, :], in_=x_view[:, t_, :])),
                          w=(("x", t_),), key=("xl0", t_))
            S.dma("sp", (lambda e, c0=c0: e.dma_start(out=cs_sb[:, 0, :], in_=cos_d[:, c0:c0 + NT])),
                  w=("cs",), key="cs")
            S.dma("sp", (lambda e, c0=c0: e.dma_start(out=cs_sb[:, 1, :], in_=sin_d[:, c0:c0 + NT])),
                  w=("cs",), key="cs")
            for li, l in enumerate(layers):
                if li > 0:
                    S.epoch += 1
                VK = ("vecs", li)
                vv = vecs[li]
                st["tlim"] = NTB
                S.tag = 'S0.c%d.l%d' % (c, li)
                if c == 0 and li == 0:
                    S.dma("sp", (lambda e, l=l: e.dma_start(out=gbc[:], in_=gpre_d[l].partition_broadcast(128))),
                          w=("gbc",), key="gbc")
                sc = next_sm(TPC)
                SMK = tuple(("sm", sc + q_) for q_ in range(TPC))
                for t_ in range(TPC):
                    act(htok[:, t_, :], x_sb[:, t_, :], AF.Square, r=(("x", t_),), w=(("htok", t_), SMK[t_]),
                        accum=small[:, sc + t_:sc + t_ + 1])
                rsqrt_small(small[:, sc:sc + TPC], small[:, sc:sc + TPC], r=SMK, w=SMK, scale=1.0 / D)
                for t_ in range(TPC):
                    xt = x_sb[:, t_, :]
                    ssq = small[:, sc + t_:sc + t_ + 1]
                    stt(htok[:, t_, :], xt, ssq, gbc[:], ALU.mult, ALU.mult, r=(("x", t_), SMK[t_], "gbc"),
                        w=(("htok", t_),))
                    b = next_ps()
                    psb = ps[:, b, :].bitcast(BF16)
                    for kc in range(8):
                        S.add("pe", (lambda e, t_=t_, kc=kc, psb=psb: e.transpose(
                            psb[:, kc * 128:(kc + 1) * 128], htok[:, t_, kc * 128:(kc + 1) * 128], ident[:])),
                            r=(("htok", t_), "ident"), w=(psk(b),))
                    cp(hT[:, :, t_ * 128:(t_ + 1) * 128], psb.rearrange("p (k t) -> p k t", k=8),
                       r=(psk(b),), w=("hT",), eng=("dve" if t_ % 2 else "act"))

                if not (c == nch - 1 and li == NL - 1):
                    l_next = layers[(li + 1) % NL]
                    S.dma("sp", (lambda e, l_next=l_next: e.dma_start(
                        out=gbc[:], in_=gpre_d[l_next].partition_broadcast(128))), w=("gbc",), key="gbc")
                if stop <= 0:
                    continue
                S.tag = 'S1.c%d.l%d' % (c, li)
                wk1, (w_cq,) = wnext("cq")
                wk2, (w_ckv, w_krs) = wnext("ckv")

                def ln_proj(wkey, wv, nun):
                    tb = [next_t() for _ in range(nun)]
                    tq = []
                    for m in range(nun):
                        b = next_ps()
                        for kc in range(8):
                            mm(ps[:, b, :], wv[:, kc, m * 128:(m + 1) * 128], hT[:, kc, :],
                               kc == 0, kc == 7, r=(wkey, "hT"), w=(psk(b),))
                        q_ = next_t()
                        tq.append(q_)
                        act(T[:, q_, :], ps[:, b, :], AF.Square, r=(psk(b),), w=(tk(q_),))
                        cp(T[:, tb[m], :], ps[:, b, :], r=(psk(b),), w=(tk(tb[m]),), eng="dve")
                    return tb, tq

                def ln_hilo(tq):
                    hl = []
                    for q_ in tq:
                        bh, bl = next_b(), next_b()
                        cp(B16[:, bh, :], T[:, q_, :], r=(tk(q_),), w=(bk(bh),), eng="dve")
                        tt(B16[:, bl, :], T[:, q_, :], B16[:, bh, :], ALU.subtract, r=(tk(q_), bk(bh)),
                           w=(bk(bl),))
                        hl.append((bh, bl))
                    return hl

                def ln_ones(hl):
                    bss = next_ps()
                    n_ = len(hl)
                    for m, (bh, bl) in enumerate(hl):
                        mm(ps[:, bss, :], ones_b[:], B16[:, bh, :], m == 0, False, r=("ones_b", bk(bh)),
                           w=(psk(bss),))
                        mm(ps[:, bss, :], ones_b[:], B16[:, bl, :], False, m == n_ - 1, r=("ones_b", bk(bl)),
                           w=(psk(bss),))
                    return bss

                def ln_fin(bss, tb, tq, gcol, dst_fn, dst_key):
                    nun = len(tb)
                    tr = tq[0]
                    act(T[:, tr, :], ps[:, bss, :], AF.Ln, r=(psk(bss),), w=(tk(tr),), bias=EPS,
                        scale=1.0 / (128 * nun))
                    act(T[:, tr, :], T[:, tr, :], AF.Exp, r=(tk(tr),), w=(tk(tr),), scale=-0.5)
                    for m in range(nun):
                        stt(dst_fn(m), T[:, tb[m], :], vv[:, gcol + m:gcol + m + 1], T[:, tr, :], ALU.mult,
                            ALU.mult, r=(tk(tb[m]), tk(tr), VK), w=(dst_key,))

                st["t"] = 0
                tb_q, tq_q = ln_proj(wk1, w_cq, 3)
                hl_q = ln_hilo(tq_q)
                tb_k, tq_k = ln_proj(wk2, w_ckv, 2)
                bss_q = ln_ones(hl_q)
                hl_k = ln_hilo(tq_k)
                bA = next_ps()
                bB = next_ps()
                for kc in range(8):
                    mm(ps[0:64, bA, :], w_ckv[:, kc, 256:320], hT[:, kc, :], kc == 0, kc == 7, r=(wk2, "hT"),
                       w=(psk(bA),))
                for kc in range(8):
                    mm(ps[0:64, bB, :], w_krs[:, kc, :], hT[:, kc, :], kc == 0, kc == 7, r=(wk2, "hT"),
                       w=(psk(bB),))
                bss_k = ln_ones(hl_k)
                ln_fin(bss_q, tb_q, tq_q, V_QG, lambda m: cqn[:, m, :], "cqn")
                ln_fin(bss_k, tb_k, tq_k, V_KVG, lambda m: ckv_c[li][:, m, c0:c0 + NT], ("ckv", li))
                t1 = next_t()
                t2 = next_t()
                tt(T[0:64, t1, :], ps[0:64, bA, :], cs_sb[:, 0, :], ALU.mult, r=(psk(bA), "cs"), w=(tk(t1),))
                tt(T[0:64, t2, :], ps[0:64, bB, :], cs_sb[:, 1, :], ALU.mult, r=(psk(bB), "cs"), w=(tk(t2),))
                tt(kr_c[li][0:64, c0:c0 + NT], T[0:64, t1, :], T[0:64, t2, :], ALU.add, r=(tk(t1), tk(t2)),
                   w=(("kr", li),))

                if stop <= 1:
                    continue
                S.tag = 'Av.c%d.l%d' % (c, li)
                wv_k = []
                wv_v = []
                for n in range(2):
                    k_, (v_,) = wnext("wv%d" % n)
                    wv_k.append(k_)
                    wv_v.append(v_)
                for t_ in range(TPC):
                    b = next_ps(2)
                    for n in range(2):
                        for kc in range(8):
                            mm(ps[:, b + n, :], hT[:, kc, t_ * 128:(t_ + 1) * 128], wv_v[n][:, kc, :], kc == 0,
                               kc == 7, r=(wv_k[n], "hT"), w=(psk(b + n),))
                        S.add("dve", (lambda e, n=n, b=b: e.bn_stats(out=bnst[:, n, :], in_=ps[:, b + n, :])),
                              r=(psk(b + n),), w=("bnst",))
                    sc = next_sm(2)
                    S.add("dve", (lambda e, sc=sc: e.bn_aggr(out=small[:, sc:sc + 2],
                                                             in_=bnst[:].rearrange("p a b -> p (a b)"))),
                          r=("bnst",), w=(("sm", sc), ("sm", sc + 1)))
                    rsqrt_small(small[:, sc + 1:sc + 2], small[:, sc + 1:sc + 2], r=(("sm", sc + 1),),
                                w=(("sm", sc + 1),), scale=1.0)
                    ts(htok[:, t_, :].rearrange("p (n f) -> p n f", n=2), ps[:, b:b + 2, :], small[:, sc:sc + 1],
                       small[:, sc + 1:sc + 2], ALU.subtract, ALU.mult,
                       r=(psk(b), psk(b + 1), ("sm", sc), ("sm", sc + 1)), w=(("htok", t_),))
                S.tag = 'S3.c%d.l%d' % (c, li)
                nk = TPC * (c + 1)
                st["pslim"] = 4
                st["ps"] = st["ps"] % 4

                def head_finalize(h_, tz_, bo_, bs_):
                    tr = next_t()
                    S.add("dve", (lambda e, tr=tr, bs_=bs_: e.reciprocal(out=T[:, tr, :], in_=ps[:, bs_, :])),
                          r=(psk(bs_),), w=(tk(tr),))
                    stt(T[:, tr, :], ps[:, bo_, :], 0.5, T[:, tr, :], ALU.mult, ALU.mult, r=(psk(bo_), tk(tr)),
                        w=(tk(tr),))
                    tt(y_b[:, h_, :], T[:, tr, :], T[:, tz_, :], ALU.mult, r=(tk(tr), tk(tz_)), w=("y_b",),
                       eng="dve")

                pending = None
                for h in range(H):
                    q0c = h * 192
                    bo, bsum = (4, 5) if h % 2 == 0 else (6, 7)
                    wkh, (wq, wqs, wkv, wzb) = wnext("h%d" % h)
                    b = next_ps()
                    for kc in range(3):
                        mm(ps[:, b, :], wq[:, kc, 0:128], cqn[:, kc, :], kc == 0, kc == 2, r=(wkh, "cqn"),
                           w=(psk(b),))
                    qp = h % 2
                    QN = ("qn", qp)
                    QR = ("qr", qp)
                    cp(qbuf[:, qp, 0, :], ps[:, b, :], r=(psk(b),), w=(QN,), eng="act")
                    bA = next_ps()
                    bB = next_ps()
                    for kc in range(3):
                        mm(ps[0:64, bA, :], wq[:, kc, 128:192], cqn[:, kc, :], kc == 0, kc == 2, r=(wkh, "cqn"),
                           w=(psk(bA),))
                    for kc in range(3):
                        mm(ps[0:64, bB, :], wqs[:, kc, :], cqn[:, kc, :], kc == 0, kc == 2, r=(wkh, "cqn"),
                           w=(psk(bB),))
                    t1 = next_t()
                    t2 = next_t()
                    tt(T[0:64, t1, :], ps[0:64, bA, :], cs_sb[:, 0, :], ALU.mult, r=(psk(bA), "cs"), w=(tk(t1),))
                    tt(T[0:64, t2, :], ps[0:64, bB, :], cs_sb[:, 1, :], ALU.mult, r=(psk(bB), "cs"), w=(tk(t2),))
                    tt(qbuf[0:64, qp, 1, :], T[0:64, t1, :], T[0:64, t2, :], ALU.add, r=(tk(t1), tk(t2)),
                       w=(QR,))
                    for kb in range(c + 1):
                        b = next_ps()
                        for kc in range(2):
                            mm(ps[:, b, :], wkv[:, kc, 0:128], ckv_c[li][:, kc, kb * NT:(kb + 1) * NT], kc == 0,
                               kc == 1, r=(wkh, ("ckv", li)), w=(psk(b),))
                        cp(k_h[:, kb * NT:(kb + 1) * NT], ps[:, b, :], r=(psk(b),), w=("k_h",),
                           eng=("act" if kb % 2 else "dve"))
                        b = next_ps()
                        for j in range(4):
                            for kc in range(2):
                                mm(ps[:, b, j * 128:(j + 1) * 128],
                                   ckv_c[li][:, kc, (4 * kb + j) * 128:(4 * kb + j + 1) * 128],
                                   wkv[:, kc, 128:256], kc == 0, kc == 1, r=(wkh, ("ckv", li)), w=(psk(b),))
                        cp(v_h[:, 4 * kb:4 * kb + 4, :], ps[:, b, :].rearrange("p (j d) -> p j d", j=4),
                           r=(psk(b),), w=("v_h",), eng=("dve" if kb % 2 else "act"))
                    bz = next_ps()
                    for kc in range(8):
                        mm(ps[:, bz, :], wzb[:, kc, :], hT[:, kc, :], kc == 0, kc == 7, r=(wkh, "hT"), w=(psk(bz),))
                    tz = next_t()
                    tanh_half(T[:, tz, :], ps[:, bz, :], r_src=(psk(bz),), w_dst=(tk(tz),))
                    stt(T[:, tz, :], T[:, tz, :], 1.0, ps[:, bz, :], ALU.add, ALU.mult, r=(psk(bz), tk(tz)),
                        w=(tk(tz),))
                    if pending is not None:
                        head_finalize(*pending)
                    LOOK = 3
                    sbank = {}

                    def scores(kt):
                        j = kt - TPC * c
                        q0 = 128 * max(j, 0)
                        b = next_ps()
                        sbank[kt] = b
                        mm(ps[:, b, q0:NT], k_h[:, kt * 128:(kt + 1) * 128], qbuf[:, qp, 0, q0:NT], True, False,
                           r=("k_h", QN), w=(psk(b),))
                        mm(ps[:, b, q0:NT], kr_c[li][:, kt * 128:(kt + 1) * 128], qbuf[:, qp, 1, q0:NT], False,
                           True, r=(("kr", li), QR), w=(psk(b),))

                    for kt in range(min(LOOK, nk)):
                        scores(kt)
                    for kt in range(nk):
                        if kt + LOOK < nk:
                            scores(kt + LOOK)
                        j = kt - TPC * c
                        q0 = 128 * max(j, 0)
                        b = sbank.pop(kt)
                        pt = next_b()
                        if j < 0:
                            act(B16[:, pt, :], ps[:, b, :], AF.Exp, r=(psk(b),), w=(bk(pt),), scale=SCALE)
                        else:
                            act(B16[:, pt, q0 + 64:NT], ps[:, b, q0 + 64:NT], AF.Exp, r=(psk(b),), w=(bk(pt),),
                                scale=SCALE)
                            act(B16[0:64, pt, q0:q0 + 64], ps[0:64, b, q0:q0 + 64], AF.Exp, r=(psk(b),),
                                w=(bk(pt),), scale=SCALE)
                            S.add("dve", (lambda e, pt=pt, q0=q0: e.memset(B16[64:128, pt, q0:q0 + 64], 0.0)),
                                  r=(), w=(bk(pt),))
                        mm(ps[:, bo, q0:NT], v_h[:, kt, :], B16[:, pt, q0:NT], kt == 0, kt == nk - 1,
                           r=("v_h", bk(pt)), w=(psk(bo),))
                        mm(ps[:, bsum, q0:NT], ones_b[:], B16[:, pt, q0:NT], kt == 0, kt == nk - 1,
                           r=("ones_b", bk(pt)), w=(psk(bsum),))
                    pending = (h, tz, bo, bsum)
                head_finalize(*pending)

                if stop <= 2:
                    continue
                S.tag = 'A.c%d.l%d' % (c, li)
                st["pslim"] = 8
                wu = {}
                for cc in range(8):
                    if cc % 4 == 0:
                        ku, (vu,) = wnext("u%d" % (cc // 4))
                        kz, (vz,) = wnext("z%d" % (cc // 4))
                    g = cc // 2
                    bs_ = next_ps()
                    for t_ in range(TPC):
                        mm(ps[:, bs_, t_ * 128:(t_ + 1) * 128], htok[:, t_, cc * 128:(cc + 1) * 128],
                           wsT_bf[li][:, g, :], True, True, r=(("htok", t_), ("wsT_bf", li)), w=(psk(bs_),))
                    bu = next_ps()
                    for kc in range(8):
                        mm(ps[:, bu, :], vu[:, kc, (cc % 4) * 128:(cc % 4 + 1) * 128], hT[:, kc, :], kc == 0, kc == 7,
                           r=(ku, "hT"), w=(psk(bu),))
                    bz = next_ps()
                    for kc in range(8):
                        mm(ps[:, bz, :], vz[:, kc, (cc % 4) * 128:(cc % 4 + 1) * 128], hT[:, kc, :], kc == 0, kc == 7,
                           r=(kz, "hT"), w=(psk(bz),))
                    ta = next_t()
                    stt(T[:, ta, :].rearrange("p (t i) -> p t i", t=4),
                        ps[:, bs_, :].rearrange("p (t i) -> p t i", t=4),
                        vv[:, V_LNG + cc:V_LNG + cc + 1],
                        Bt[li][:, cc, :].unsqueeze(1).broadcast_to([128, 4, 128]),
                        ALU.mult, ALU.add, r=(psk(bs_), VK, ("Bt", li)), w=(tk(ta),))
                    stt(T[:, ta, :], ps[:, bu, :], 0.5, T[:, ta, :], ALU.mult, ALU.mult, r=(psk(bu), tk(ta)),
                        w=(tk(ta),))
                    tz = next_t()
                    tanh_half(T[:, tz, :], ps[:, bz, :], r_src=(psk(bz),), w_dst=(tk(tz),))
                    stt(T[:, tz, :], T[:, tz, :], 1.0, ps[:, bz, :], ALU.add, ALU.mult, r=(psk(bz), tk(tz)),
                        w=(tk(tz),))
                    tt(y_a[:, cc, :], T[:, ta, :], T[:, tz, :], ALU.mult, r=(tk(ta), tk(tz)), w=("y_a",),
                       eng="dve")

                if stop <= 3:
                    continue
                S.tag = 'C.c%d.l%d' % (c, li)
                for hf in range(2):
                    for uu in range(5):
                        u = hf * 5 + uu
                        if uu == 0:
                            kx, (vx,) = wnext("xc%da" % hf)
                        elif uu == 3:
                            kx, (vx,) = wnext("xc%db" % hf)
                        uo = uu if uu < 3 else uu - 3
                        b = next_ps()
                        for kc in range(8):
                            mm(ps[:, b, :], vx[:, kc, uo * 128:(uo + 1) * 128], hT[:, kc, :], kc == 0, kc == 7,
                               r=(kx, "hT"), w=(psk(b),))
                        rb = uu % 2
                        RK = ("raw", rb)
                        cp(raw[:, rb, 4:NT + 4], ps[:, b, :], r=(psk(b),), w=(RK,), eng="act")
                        cp(raw[:, rb, 1:4], hist[li][:, u, 1:4], r=(("hist", li),), w=(RK,), eng="dve")
                        cw = V_CW + u * 4
                        act(xcf[:, uu, :], ps[:, b, :], AF.Identity, r=(psk(b), VK), w=(("xcf", uu),),
                            scale=vv[:, cw + 3:cw + 4], bias=vv[:, V_CB + u:V_CB + u + 1])
                        for k in range(3):
                            stt(xcf[:, uu, :], raw[:, rb, 1 + k:1 + k + NT], vv[:, cw + k:cw + k + 1], xcf[:, uu, :],
                                ALU.mult, ALU.add, r=(RK, VK, ("xcf", uu)), w=(("xcf", uu),))
                        cp(hist[li][:, u, 1:4], raw[:, rb, NT + 1:NT + 4], r=(RK,), w=(("hist", li),), eng="dve")
                        cp(xcb[:, uu, :], xcf[:, uu, :], r=(("xcf", uu),), w=(("xcb", uu),), eng="dve")
                    kg, (vga, vgx) = wnext("g%d" % hf)
                    NK = ("nca", li)
                    n = 5
                    st["t"] = 0
                    tA, tI = next_t(n), next_t(n)
                    t_r = next_t()
                    for uu in range(n):
                        u = hf * 5 + uu
                        prs = [(qi, ki) for qi, (ou, ki) in enumerate(GATE_PAIRS) if ou == uu]
                        br = next_ps()
                        for n_, (qi, ki) in enumerate(prs):
                            mm(ps[:, br, :], vga[:, qi, :], xcb[:, ki, :], n_ == 0, n_ == len(prs) - 1,
                               r=(kg, ("xcb", ki)), w=(psk(br),))
                        bi = next_ps()
                        for n_, (qi, ki) in enumerate(prs):
                            mm(ps[:, bi, :], vgx[:, qi, :], xcb[:, ki, :], n_ == 0, n_ == len(prs) - 1,
                               r=(kg, ("xcb", ki)), w=(psk(bi),))
                        tanh_half(T[:, t_r, :], ps[:, br, :], r_src=(psk(br), NK), w_dst=(tk(t_r),),
                                  bias=nca[li][:, 20 + u:21 + u])
                        tanh_half(T[:, tI + uu, :], ps[:, bi, :], r_src=(psk(bi), NK), w_dst=(tk(tI + uu),),
                                  bias=nca[li][:, 30 + u:31 + u])
                        act(T[:, tA + uu, :], T[:, t_r, :], AF.Exp, r=(tk(t_r), NK), w=(tk(tA + uu),),
                            scale=nca[li][:, u:u + 1], bias=nca[li][:, u:u + 1])
                    bzs = []
                    for uu in range(n):
                        if uu == 0:
                            kzc, (vzc,) = wnext("zc%da" % hf)
                        elif uu == 3:
                            kzc, (vzc,) = wnext("zc%db" % hf)
                        uo = uu if uu < 3 else uu - 3
                        bz = next_ps()
                        bzs.append(bz)
                        for kc in range(8):
                            mm(ps[:, bz, :], vzc[:, kc, uo * 128:(uo + 1) * 128], hT[:, kc, :], kc == 0, kc == 7,
                               r=(kzc, "hT"), w=(psk(bz),))
                    AK = tuple(tk(tA + i_) for i_ in range(n))
                    IK = tuple(tk(tI + i_) for i_ in range(n))
                    XK = tuple(("xcf", uu) for uu in range(n))
                    Av_ = T[:, tA:tA + n, :]
                    Iv = T[:, tI:tI + n, :]
                    Xv = xcf[:, 0:n, :]
                    stt(Xv, Iv, 1.0, Xv, ALU.add, ALU.mult, r=IK + XK, w=XK)
                    act(Iv, Av_, AF.Square, r=AK, w=IK)
                    ts(Iv, Iv, 0.9999999, -1.0, ALU.min, ALU.mult, r=IK, w=IK)
                    act(Iv, Iv, AF.Ln, r=IK, w=IK, bias=1.0)
                    act(Iv, Iv, AF.Exp, r=IK + ("lnq",), w=IK, scale=0.5, bias=LN_QUARTER[:, 0:1])
                    tt(Iv, Iv, Xv, ALU.mult, r=IK + XK, w=IK)
                    for uu in range(n):
                        u = hf * 5 + uu
                        HK = ("hst", li, u)
                        S.add("dve", (lambda e, uu=uu, a_=tA + uu, m_=tI + uu, li=li, u=u:
                                      e.tensor_tensor_scan(out=xcf[:, uu, :], data0=T[:, a_, :], data1=T[:, m_, :],
                                                           initial=hst[li][:, u:u + 1], op0=ALU.mult,
                                                           op1=ALU.add)),
                              r=(tk(tA + uu), tk(tI + uu), HK), w=(("xcf", uu),))
                        cp(hst[li][:, u:u + 1], xcf[:, uu, NT - 1:NT], r=(("xcf", uu),), w=(HK,), eng="dve")
                    for uu in range(n):
                        u = hf * 5 + uu
                        bz = bzs[uu]
                        tz = tA + uu
                        tanh_half(T[:, tz, :], ps[:, bz, :], r_src=(psk(bz),), w_dst=(tk(tz),))
                        stt(T[:, tz, :], T[:, tz, :], 1.0, ps[:, bz, :], ALU.add, ALU.mult, r=(psk(bz), tk(tz)),
                            w=(tk(tz),))
                        tt(y_c[:, u, :], xcf[:, uu, :], T[:, tz, :], ALU.mult, r=(("xcf", uu), tk(tz)),
                           w=("y_c",), eng="pool")

                if stop <= 4:
                    continue
                S.tag = 'M.c%d.l%d' % (c, li)
                st["tlim"] = NTB - 2
                st["t"] = 0
                S.dma("sp", (lambda e, l=l: e.dma_start(out=T[:, NTB - 2:NTB, :].rearrange("p a b -> p (a b)"),
                                                        in_=gpost_d[l].partition_broadcast(128))),
                      w=(tk(NTB - 2), tk(NTB - 1)), key="gpo")
                for dc in range(8):
                    kgt, vg = wnext("G%d" % dc)
                    kpr, (vpa, vpb, vpc) = wnext("P%d" % dc)
                    t_acc = next_t()
                    mixers = [(y_a, "y_a", 8, vpa), (y_b, "y_b", 8, vpb), (y_c, "y_c", 10, vpc)]
                    bgs, t_ss = [], []
                    for m in range(3):
                        bg = next_ps()
                        for kc in range(8):
                            mm(ps[:, bg, :], vg[m][:, kc, :], hT[:, kc, :], kc == 0, kc == 7, r=(kgt, "hT"),
                               w=(psk(bg),))
                        t_s = next_t()
                        tanh_half(T[:, t_s, :], ps[:, bg, :], r_src=(psk(bg),), w_dst=(tk(t_s),))
                        bgs.append(bg)
                        t_ss.append(t_s)
                    MGK = ("htok", dc // 2)
                    for m, (yb, ykey, nkc, wp) in enumerate(mixers):
                        bp = next_ps()
                        for k in range(nkc):
                            mm(ps[:, bp, :], wp[:, k, :], yb[:, k, :], k == 0, k == nkc - 1, r=(kpr, ykey),
                               w=(psk(bp),))
                        t_s = t_ss[m]
                        if m == 0:
                            stt(T[:, t_acc, :], T[:, t_s, :], 1.0, ps[:, bp, :], ALU.add, ALU.mult,
                                r=(psk(bp), tk(t_s)), w=(tk(t_acc),))
                        else:
                            stt(T[:, t_s, :], T[:, t_s, :], 1.0, ps[:, bp, :], ALU.add, ALU.mult,
                                r=(psk(bp), tk(t_s)), w=(tk(t_s),))
                            if m == 1:
                                tt(T[:, t_acc, :], T[:, t_acc, :], T[:, t_s, :], ALU.add, r=(tk(t_acc), tk(t_s)),
                                   w=(tk(t_acc),), eng="dve")
                            else:
                                tt(mrg[:, dc, :], T[:, t_acc, :], T[:, t_s, :], ALU.add, r=(tk(t_acc), tk(t_s)),
                                   w=(MGK,), eng="dve")

                if stop <= 5:
                    continue
                S.tag = 'O.c%d.l%d' % (c, li)
                wo_k = []
                wo_v = []
                for n in range(2):
                    k_, (v_,) = wnext("wo%d" % n)
                    wo_k.append(k_)
                    wo_v.append(v_)
                for t_ in range(TPC):
                    b = next_ps(2)
                    for n in range(2):
                        for kc in range(8):
                            mm(ps[:, b + n, :], mrg[:, kc, t_ * 128:(t_ + 1) * 128], wo_v[n][:, kc, :], kc == 0,
                               kc == 7, r=(wo_k[n],) + tuple(("htok", q_) for q_ in range(4)), w=(psk(b + n),))
                    sc = next_sm()
                    ssq = small[:, sc:sc + 1]
                    act(y_a[:, 2 * t_:2 * t_ + 2, :], ps[:, b:b + 2, :], AF.Square,
                        r=(psk(b), psk(b + 1)), w=("y_a", ("sm", sc)), accum=ssq)
                    rsqrt_small(ssq, ssq, r=(("sm", sc),), w=(("sm", sc),), scale=1.0 / D, eps=4.0 * EPS)
                    t2 = next_t(2)
                    stt(T[:, t2:t2 + 2, :], ps[:, b:b + 2, :], ssq, T[:, NTB - 2:NTB, :],
                        ALU.mult, ALU.mult, r=(psk(b), psk(b + 1), ("sm", sc), tk(NTB - 2), tk(NTB - 1)),
                        w=(tk(t2), tk(t2 + 1)))
                    tt(x_sb[:, t_, :].rearrange("p (n f) -> p n f", n=2),
                       x_sb[:, t_, :].rearrange("p (n f) -> p n f", n=2), T[:, t2:t2 + 2, :], ALU.add,
                       r=(("x", t_), tk(t2), tk(t2 + 1)), w=(("x", t_),), eng="dve")
                    if li == NL - 1:
                        S.dma("sp", (lambda e, c=c, t_=t_: e.dma_start(out=out_view[:, c * TPC + t_, :],
                                                                        in_=x_sb[:, t_, :])),
                              r=(("x", t_),), w=(("outd", t_),), key=("xo", t_))
                        if c + 1 < nch:
                            S.dma("pool", (lambda e, c=c, t_=t_: e.dma_start(out=x_sb[:, t_, :],
                                                                              in_=x_view[:, (c + 1) * TPC + t_, :])),
                                  w=(("x", t_),), key=("xl", t_))

        n_out = {t_: S.dma_cnt.get(("xo", t_), 0) for t_ in range(TPC)}

        sig, rank = S.analyse()
        nep = S.n_epochs()
        eng_sems = {e: [es.enter_context(nc.semaphore(f"s_{e}_{i}")) for i in range(nep)] for e in Sched.ENGS}
        dma_sems = {k: es.enter_context(nc.semaphore(f"d_{i}")) for i, k in enumerate(S.dma_cnt.keys())}
        block = es.enter_context(nc.Block())

        @block.tensor
        def _(e):
            S.emit("pe", e, eng_sems, dma_sems, sig, rank)

        @block.scalar
        def _(e):
            S.emit("act", e, eng_sems, dma_sems, sig, rank)

        @block.vector
        def _(e):
            S.emit("dve", e, eng_sems, dma_sems, sig, rank)

        @block.gpsimd
        def _(e):
            S.emit("pool", e, eng_sems, dma_sems, sig, rank)

        @block.sync
        def _(e):
            S.emit("sp", e, eng_sems, dma_sems, sig, rank)
            for t_ in range(TPC):
                if n_out[t_]:
                    e.wait_ge(dma_sems[("xo", t_)], 16 * n_out[t_])

    nc._sched = S
    return nc


_PROG_CACHE = {}


def _host_layout(inp):
    L = 2
    f32 = np.float32
    def pad_gate(w):
        full = np.zeros((L, 1280, 1280), f32)
        for hb in range(16):
            full[:, hb * 80:(hb + 1) * 80, hb * 80:(hb + 1) * 80] = w[:, hb]
        tiles = np.zeros((L, 26, 128, 128), f32)
        for hf in range(2):
            for qi, (ou, ki) in enumerate(GATE_PAIRS):
                r0 = (hf * 5 + ki) * 128
                c0 = (hf * 5 + ou) * 128
                tiles[:, hf * 13 + qi] = full[:, r0:r0 + 128, c0:c0 + 128]
        return tiles

    vecs = np.zeros((L, 128, NV), f32)

    def chunks(v, n):
        return np.ascontiguousarray(v.reshape(L, n, 128).transpose(0, 2, 1))

    vecs[:, :, V_LNG:V_LNG + 8] = chunks(inp["gm_ln_g"], 8)
    vecs[:, :, V_LNB:V_LNB + 8] = chunks(inp["gm_ln_b"], 8)
    vecs[:, :, V_QG:V_QG + 3] = chunks(inp["mla_q_norm_g"], 3)
    vecs[:, :, V_KVG:V_KVG + 2] = chunks(inp["mla_kv_norm_g"], 2)
    cw = inp["lru_conv_w"].reshape(L, 4, 10, 128).transpose(0, 3, 2, 1)
    vecs[:, :, V_CW:V_CW + 40] = cw.reshape(L, 128, 40)
    vecs[:, :, V_CB:V_CB + 10] = chunks(inp["lru_conv_b"], 10)
    vecs[:, :, V_BA:V_BA + 10] = chunks(inp["lru_b_a"], 10)
    vecs[:, :, V_BX:V_BX + 10] = chunks(inp["lru_b_x"], 10)
    vecs[:, :, V_LAM:V_LAM + 10] = chunks(inp["lru_lambda"], 10)

    pos = np.arange(SEQ, dtype=f32)
    inv_freq = (10000.0 ** (-np.arange(0, 64, 2, dtype=f32) / f32(64))).astype(f32)
    ang = (pos[:, None] * inv_freq[None, :]).astype(f32)
    cos = np.cos(ang).astype(f32).T
    sin = np.sin(ang).astype(f32).T
    arrs = dict(
        w_in=np.asarray(inp["w_in"], f32),
        w_uq=np.asarray(inp["mla_w_uq"], f32),
        w_ukv=np.asarray(inp["mla_w_ukv"], f32),
        w_pa=np.asarray(inp["w_proj_a"], f32),
        w_pb=np.asarray(inp["w_proj_b"], f32),
        w_pc=np.asarray(inp["w_proj_c"], f32),
        w_out=np.asarray(inp["w_out"], f32),
        wa_pad=pad_gate(np.asarray(inp["lru_w_a"], f32)),
        wx_pad=pad_gate(np.asarray(inp["lru_w_x"], f32)),
    )
    shared = dict(
        wimg=build_slot_images(arrs),
        wsT=np.ascontiguousarray(np.asarray(inp["gm_ws"], f32).transpose(0, 3, 1, 2)),
        bs=np.ascontiguousarray(np.asarray(inp["gm_bs"], f32).reshape(L, 512)),
        vecs=vecs,
        gpre=np.ascontiguousarray(inp["pre_norm_g"], f32),
        gpost=np.ascontiguousarray(inp["post_norm_g"], f32),
        cos2=np.ascontiguousarray(np.concatenate([cos, cos], 0)),
        sin2=np.ascontiguousarray(np.concatenate([-sin, sin], 0)),
        ident=np.eye(128, dtype=f32),
    )
    return shared


def _run(layers, x, shared):
    key = tuple(layers)
    if key not in _PROG_CACHE:
        _PROG_CACHE[key] = build_program(list(layers))
    nc = _PROG_CACHE[key]
    in_maps = []
    for b in range(8):
        m = dict(shared)
        m["x"] = np.ascontiguousarray(x[b], np.float32)
        in_maps.append(m)
    res = run_bass_kernel_spmd(nc, in_maps, core_ids=list(range(8)))
    return np.stack([np.asarray(r["out"], np.float32) for r in res.results], 0)


FUSED = True


def kernel(**inputs):
    inp = {k: np.asarray(v) for k, v in inputs.items()}
    shared = _host_layout(inp)
    x = np.asarray(inp["x"], np.float32)
    if FUSED:
        return _run((0, 1), x, shared)
    y = _run((0,), x, shared)
    return _run((1,), y, shared)
```

```python
import math
from contextlib import ExitStack

import numpy as np
import concourse.bass as bass
import concourse.mybir as mybir
from concourse.bass_utils import run_bass_kernel_spmd

F32 = mybir.dt.float32
BF16 = mybir.dt.bfloat16
ALU = mybir.AluOpType
AF = mybir.ActivationFunctionType

D = 1024
SEQ = 2048
NT = 512
NCH = SEQ // NT
TPC = NT // 128
EPS = 1e-6
H = 8
N_IN = 10432
O_U, O_V, O_ZA, O_CQ, O_CKV, O_KR, O_ZB, O_XC, O_ZC, O_GA, O_GB, O_GC = (
    0, 1024, 2048, 3072, 3456, 3712, 3776, 4800, 6080, 7360, 8384, 9408)
SCALE = 1.0 / math.sqrt(192.0)
V_LNG, V_LNB, V_QG, V_KVG, V_CW, V_CB, V_BA, V_BX, V_LAM = 0, 8, 16, 19, 21, 61, 71, 81, 91
NV = 101
GATE_PAIRS = [(0, 0), (0, 1), (1, 0), (1, 1), (1, 2), (2, 1), (2, 2), (2, 3), (3, 2), (3, 3), (3, 4),
              (4, 3), (4, 4)]
WSLOT = 4096
NSLOT = 4
LOOKAHEAD = 2
NTB = 12
NB16 = 8


def req_plan():
    def cols(arr, c0, c1):
        return ("cols", arr, c0, c1)
    P = []
    P.append(("cq", [(0, [8, 384], cols("w_in", O_CQ, O_CQ + 384))]))
    P.append(("ckv", [(0, [8, 320], cols("w_in", O_CKV, O_CKV + 320)), (8 * 320, [8, 64], ("ropesw", "w_in", O_KR))]))
    for n in range(2):
        P.append(("wv%d" % n, [(0, [8, 512], cols("w_in", O_V + n * 512, O_V + (n + 1) * 512))]))
    for h in range(H):
        q0c = h * 192
        P.append(("h%d" % h, [
            (0, [3, 192], cols("w_uq", q0c, q0c + 192)),
            (576, [3, 64], ("ropesw", "w_uq", q0c + 128)),
            (768, [2, 256], cols("w_ukv", h * 256, (h + 1) * 256)),
            (1280, [8, 128], cols("w_in", O_ZB + h * 128, O_ZB + (h + 1) * 128))]))
    for n in range(2):
        P.append(("u%d" % n, [(0, [8, 512], cols("w_in", O_U + n * 512, O_U + (n + 1) * 512))]))
        P.append(("z%d" % n, [(0, [8, 512], cols("w_in", O_ZA + n * 512, O_ZA + (n + 1) * 512))]))
    for hf in range(2):
        P.append(("xc%da" % hf, [(0, [8, 384], cols("w_in", O_XC + hf * 640, O_XC + hf * 640 + 384))]))
        P.append(("xc%db" % hf, [(0, [8, 256], cols("w_in", O_XC + hf * 640 + 384, O_XC + (hf + 1) * 640))]))
        P.append(("g%d" % hf, [(0, [13, 128], ("pad", "wa_pad", hf)), (13 * 128, [13, 128], ("pad", "wx_pad", hf))]))
        P.append(("zc%da" % hf, [(0, [8, 384], cols("w_in", O_ZC + hf * 640, O_ZC + hf * 640 + 384))]))
        P.append(("zc%db" % hf, [(0, [8, 256], cols("w_in", O_ZC + hf * 640 + 384, O_ZC + (hf + 1) * 640))]))
    for dc in range(8):
        P.append(("G%d" % dc, [(m_ * 1024, [8, 128], cols("w_in", O_GA + m_ * 1024 + dc * 128,
                                                            O_GA + m_ * 1024 + (dc + 1) * 128)) for m_ in range(3)]))
        P.append(("P%d" % dc, [(0, [8, 128], cols("w_pa", dc * 128, (dc + 1) * 128)),
                               (1024, [8, 128], cols("w_pb", dc * 128, (dc + 1) * 128)),
                               (2048, [10, 128], cols("w_pc", dc * 128, (dc + 1) * 128))]))
    for n in range(2):
        P.append(("wo%d" % n, [(0, [8, 512], cols("w_out", n * 512, (n + 1) * 512))]))
    return P


def req_used(parts):
    return max(off + int(np.prod(shp)) for off, shp, _ in parts)


def build_slot_images(arrs):
    plan = req_plan()
    L = 2
    img = np.zeros((L, len(plan), 128, WSLOT), np.float32)

    def kp(a):
        k = a.shape[0] // 128
        return a.reshape(k, 128, a.shape[1]).transpose(1, 0, 2).reshape(128, -1)

    for l in range(L):
        for j, (name, parts) in enumerate(plan):
            for off, shp, spec in parts:
                n = int(np.prod(shp))
                if spec[0] == "cols":
                    data = kp(arrs[spec[1]][l][:, spec[2]:spec[3]])
                elif spec[0] == "ropesw":
                    a = arrs[spec[1]][l]
                    c0 = spec[2]
                    data = kp(np.concatenate([a[:, c0 + 32:c0 + 64], a[:, c0:c0 + 32]], axis=1))
                else:
                    a = arrs[spec[1]][l, spec[2] * 13:(spec[2] + 1) * 13]
                    data = a.transpose(1, 0, 2).reshape(128, -1)
                assert data.shape == (128, n), (name, data.shape, n)
                img[l, j, :, off:off + n] = data
    return img


class Sched:
    ENGS = ("pe", "act", "dve", "pool", "sp")

    def __init__(self):
        self.ops = {e: [] for e in self.ENGS}
        self.last_w = {}
        self.readers = {}
        self.dma_cnt = {}
        self.epoch = 0
        self.tag = 'setup'

    def _deps(self, reads, writes):
        d = {}

        def put(t):
            k, v = t
            if d.get(k, -1) < v:
                d[k] = v

        for k in reads:
            if k in self.last_w:
                put(self.last_w[k])
        for k in writes:
            if k in self.last_w:
                put(self.last_w[k])
            for kk, vv in self.readers.get(k, {}).items():
                put((kk, vv))
        return d

    def _commit(self, tok, reads, writes):
        k, v = tok
        for b in reads:
            r = self.readers.setdefault(b, {})
            if r.get(k, -1) < v:
                r[k] = v
        for b in writes:
            self.last_w[b] = tok
            self.readers[b] = {}

    def add(self, eng, fn, r=(), w=()):
        pr = [k for k in r if isinstance(k, tuple) and k and k[0] == "ps"]
        if pr:
            r = [k for k in r if k not in pr]
            w = list(w) + pr
        deps = self._deps(r, w)
        if eng == "pe":
            deps.pop(("e", "pe"), None)
        idx = len(self.ops[eng])
        self.ops[eng].append(dict(fn=fn, deps=deps, epoch=self.epoch, dma=None, tag=self.tag))
        self._commit((("e", eng), idx), r, w)

    def dma(self, q, fn, r=(), w=(), key=None, first=True):
        deps = self._deps(r, w)
        if not first:
            deps.pop(("d", key), None)
        n = self.dma_cnt.get(key, 0) + 1
        self.dma_cnt[key] = n
        self.ops[q].append(dict(fn=fn, deps=deps, epoch=self.epoch, dma=key, tag=self.tag))
        self._commit((("d", key), n * 16), r, w)

    def n_epochs(self):
        return self.epoch + 1

    def emit(self, eng, handle, eng_sems, dma_sems, sig, rank):
        waited = {}
        for idx, op in enumerate(self.ops[eng]):
            for (kind, x), v in op["deps"].items():
                if kind == "e":
                    ep, val = rank[(x, v)]
                    sem = eng_sems[x][ep]
                else:
                    sem, val = dma_sems[x], v
                key = id(sem)
                if waited.get(key, 0) < val:
                    handle.wait_ge(sem, val)
                    waited[key] = val
            ins = op["fn"](handle)
            if op["dma"] is not None:
                ins.then_inc(dma_sems[op["dma"]], 16)
            elif idx in sig[eng]:
                ins.then_inc(eng_sems[eng][op["epoch"]], 1)

    def analyse(self):
        sig = {e: set() for e in self.ENGS}
        for e in self.ENGS:
            for op in self.ops[e]:
                for (kind, x), v in op["deps"].items():
                    if kind == "e":
                        sig[x].add(v)
        rank = {}
        for e in self.ENGS:
            cnt = {}
            for idx, op in enumerate(self.ops[e]):
                if idx in sig[e]:
                    ep = op["epoch"]
                    cnt[ep] = cnt.get(ep, 0) + 1
                    rank[(e, idx)] = (ep, cnt[ep])
        return sig, rank


def build_program(layers, debug=False, nch=NCH, stop=9):
    nc = bass.Bass("TRN2", target_bir_lowering=False)
    NL = len(layers)
    L2 = 2

    def din(name, shape, dt=F32):
        return nc.dram_tensor(name, list(shape), dt, kind="ExternalInput").ap()

    x_d = din("x", [SEQ, D])
    out_d = nc.dram_tensor("out", [SEQ, D], F32, kind="ExternalOutput").ap()
    PLAN = req_plan()
    NREQ = len(PLAN)
    wimg_d = din("wimg", [L2, NREQ, 128, WSLOT])
    wsT_d = din("wsT", [L2, 128, 4, 128])
    bs_d = din("bs", [L2, 512])
    vecs_d = din("vecs", [L2, 128, NV])
    gpre_d = din("gpre", [L2, D])
    gpost_d = din("gpost", [L2, D])
    cos_d = din("cos2", [64, SEQ])
    sin_d = din("sin2", [64, SEQ])
    ident_d = din("ident", [128, 128])

    S = Sched()
    es = ExitStack()

    def sb(name, shape, dt):
        return es.enter_context(nc.sbuf_tensor(name, list(shape), dt))

    with es:
        ps = es.enter_context(nc.psum_tensor("ps", [128, 8, 512], F32))
        x_sb = sb("x_sb", [128, TPC, D], F32)
        ckv_c = [sb(f"ckv{i}", [128, 2, SEQ], BF16) for i in range(NL)]
        kr_c = [sb(f"kr{i}", [128, SEQ], BF16) for i in range(NL)]
        hist = [sb(f"hist{i}", [128, 10, 4], F32) for i in range(NL)]
        hst = [sb(f"hst{i}", [128, 10], F32) for i in range(NL)]
        vecs = [sb(f"vecs{i}", [128, NV], F32) for i in range(NL)]
        nca = [sb(f"nca{i}", [128, 40], F32) for i in range(NL)]
        Bt = [sb(f"Bt{i}", [128, 8, 128], F32) for i in range(NL)]
        wsT_bf = [sb(f"wsTb{i}", [128, 4, 128], BF16) for i in range(NL)]
        ident = sb("ident_sb", [128, 128], BF16)
        ones_f = sb("ones_f", [128, 128], F32)
        ones_b = sb("ones_b", [128, 128], BF16)
        gbc = sb("gbc", [128, D], F32)
        cs_sb = sb("cs_sb", [64, 2, NT], F32)
        ring = [sb(f"ring{i}", [128, WSLOT], BF16) for i in range(NSLOT)]
        hT = sb("hT", [128, 8, NT], BF16)
        T = sb("T", [128, NTB, NT], F32)
        B16 = sb("B16", [128, NB16, NT], BF16)
        htok = sb("htok", [128, 4, D], BF16)
        qbuf = sb("qbuf", [128, 2, 2, NT], BF16)
        small = sb("small", [128, 64], F32)
        LN_QUARTER = sb("lnq", [128, 1], F32)
        bnst = sb("bnst", [128, 2, 6], F32)
        cqn = sb("cqn", [128, 3, NT], BF16)
        k_h = sb("k_h", [128, SEQ], BF16)
        v_h = sb("v_h", [128, 16, 128], BF16)
        y_a = sb("y_a", [128, 8, NT], BF16)
        y_b = sb("y_b", [128, 8, NT], BF16)
        y_c = sb("y_c", [128, 10, NT], BF16)
        xcf = sb("xcf", [128, 5, NT], F32)
        xcb = sb("xcb", [128, 5, NT], BF16)
        raw = sb("raw", [128, 2, NT + 4], F32)

        mrg = htok[:].rearrange("p t (a n) -> p (t a) n", a=2)
        st = dict(ps=0, t=0, b=0, ring=0, sm=0, pslim=8, tlim=NTB)

        def next_ps(n=1):
            b = st["ps"]
            if n == 2 and b % 2 == 1:
                b += 1
            lim = st["pslim"]
            if b + n > lim:
                b = 0
            st["ps"] = (b + n) % lim
            return b

        def psk(b):
            return ("ps", b)

        def next_t(n=1):
            b = st["t"]
            lim = st["tlim"]
            if b + n > lim:
                b = 0
            st["t"] = (b + n) % lim
            return b

        def tk(b):
            return ("T", b)

        def next_b():
            b = st["b"]
            st["b"] = (b + 1) % NB16
            return b

        def bk(b):
            return ("B", b)

        def next_sm(n=1):
            b = st["sm"]
            if b + n > 64:
                b = 0
            st["sm"] = (b + n) % 64
            return b

        wreqs = []
        for c_ in range(nch):
            for l_ in layers:
                for j_, (name_, parts_) in enumerate(PLAN):
                    wreqs.append((name_, parts_, l_, j_))
        wst = dict(cur=0, issued=0, views={})

        def wissue(i):
            name, parts, l_, j_ = wreqs[i]
            s = i % NSLOT
            key = ("w", s)
            views = []
            for off, shp, _spec in parts:
                n = int(np.prod(shp))
                v = ring[s][:, off:off + n]
                if len(shp) == 2:
                    v = v.rearrange("p (a b) -> p a b", a=shp[0], b=shp[1])
                views.append(v)
            used = req_used(parts)
            S.dma("pool", (lambda e, s=s, l_=l_, j_=j_, used=used: e.dma_start(
                out=ring[s][:, 0:used], in_=wimg_d[l_, j_, :, 0:used], max_dma_last_dim=8192)),
                r=(), w=(key,), key=key)
            wst["views"][i] = (key, views)

        def wnext(name):
            i = wst["cur"]
            assert wreqs[i][0] == name, (wreqs[i][0], name)
            while wst["issued"] <= min(i + LOOKAHEAD, len(wreqs) - 1):
                wissue(wst["issued"])
                wst["issued"] += 1
            wst["cur"] = i + 1
            return wst["views"].pop(i)

        def mm(out, lhsT, rhs, start, stop, r, w):
            S.add("pe", (lambda e: e.matmul(out, lhsT, rhs, start=start, stop=stop)), r=r, w=w)

        def act(out, in_, func, r, w, bias=None, scale=None, accum=None):
            kw = {}
            if bias is not None:
                kw["bias"] = bias
            if scale is not None:
                kw["scale"] = scale
            if accum is not None:
                kw["accum_out"] = accum
            S.add("act", (lambda e: e.activation(out=out, in_=in_, func=func, **kw)), r=r, w=w)

        def sigmoid3(dst, src, r_src, w_dst, scale=-1.0, bias=None):
            act(dst, src, AF.Exp, r=r_src, w=w_dst, scale=scale, bias=bias)
            act(dst, dst, AF.Ln, r=w_dst, w=w_dst, bias=1.0)
            act(dst, dst, AF.Exp, r=w_dst, w=w_dst, scale=-1.0)

        def tanh_half(dst, src, r_src, w_dst, bias=None):
            act(dst, src, AF.Tanh, r=r_src, w=w_dst, scale=0.5, bias=bias)

        def dve(fn, r, w):
            S.add("dve", fn, r=r, w=w)

        def tt(out, in0, in1, op, r, w, eng="dve"):
            S.add(eng, (lambda e: e.tensor_tensor(out=out, in0=in0, in1=in1, op=op)), r=r, w=w)

        def stt(out, in0, scalar, in1, op0, op1, r, w):
            S.add("dve", (lambda e: e.scalar_tensor_tensor(out=out, in0=in0, scalar=scalar, in1=in1,
                                                           op0=op0, op1=op1)), r=r, w=w)

        def ts(out, in0, s1, s2, op0, op1, r, w, eng="dve"):
            if s2 is None:
                S.add(eng, (lambda e: e.tensor_scalar(out=out, in0=in0, scalar1=s1, scalar2=None, op0=op0)),
                      r=r, w=w)
            else:
                S.add(eng, (lambda e: e.tensor_scalar(out=out, in0=in0, scalar1=s1, scalar2=s2, op0=op0,
                                                      op1=op1)), r=r, w=w)

        def cp(out, in_, r, w, eng):
            if eng == "act":
                S.add("act", (lambda e: e.activation(out=out, in_=in_, func=AF.Copy)), r=r, w=w)
            else:
                S.add(eng, (lambda e: e.tensor_copy(out=out, in_=in_)), r=r, w=w)

        def rsqrt_small(dst, src, r, w, scale, eps=EPS):
            act(dst, src, AF.Ln, r=r, w=w, bias=eps, scale=scale)
            act(dst, dst, AF.Exp, r=w, w=w, scale=-0.5)

        wsT_f = T[:, 0, :].rearrange("p (g i) -> p g i", g=4)
        bs_bc = T[:, 1, :].rearrange("p (g i) -> p g i", g=4)
        x_view = x_d.rearrange("(t p) d -> p t d", p=128)
        for t_ in range(TPC):
            S.dma("sp", (lambda e, t_=t_: e.dma_start(out=x_sb[:, t_, :], in_=x_view[:, t_, :])),
                  w=(("x", t_),), key=("xl0", t_))
        S.dma("sp", (lambda e: e.dma_start(out=gbc[:], in_=gpre_d[layers[0]].partition_broadcast(128))),
              w=("gbc",), key="gbc")
        S.dma("sp", (lambda e: e.dma_start(out=cs_sb[:, 0, :], in_=cos_d[:, 0:NT])), w=("cs",), key="cs")
        S.dma("sp", (lambda e: e.dma_start(out=cs_sb[:, 1, :], in_=sin_d[:, 0:NT])), w=("cs",), key="cs")
        S.dma("pool", lambda e: e.dma_start(out=ident[:], in_=ident_d[:, :]), w=("ident",), key="c0")
        S.add("dve", lambda e: e.memset(ones_f[:], 1.0), w=("ones_f",))
        S.add("dve", lambda e: e.memset(ones_b[:], 1.0), w=("ones_b",))
        S.add("dve", lambda e: e.memset(LN_QUARTER[:], -1.3862943611198906), w=("lnq",))
        for qp_ in range(2):
            S.add("dve", (lambda e, qp_=qp_: e.memset(qbuf[64:128, qp_, 1, :], 0.0)), w=(("qr", qp_),))
        for li_ in range(NL):
            S.add("dve", (lambda e, li_=li_: e.memset(kr_c[li_][64:128, :], 0.0)), w=(("kr", li_),))
        for li, l in enumerate(layers):
            S.add("dve", (lambda e, li=li: e.memset(hist[li][:], 0.0)), w=(("hist", li),))
            S.add("dve", (lambda e, li=li: e.memset(hst[li][:], 0.0)), w=(("hst", li),))
            S.dma("sp", (lambda e, li=li, l=l: e.dma_start(out=vecs[li][:], in_=vecs_d[l])), w=(("vecs", li),),
                  key=("c1", li))
            S.dma("sp", (lambda e, l=l: e.dma_start(out=wsT_f, in_=wsT_d[l])), w=(tk(0),), key=("c2", li))
            S.dma("sp", (lambda e, l=l: e.dma_start(
                out=T[:, 1, :], in_=bs_d[l].partition_broadcast(128))),
                w=(tk(1),), key=("c3", li))
            S.add("dve", lambda e: e.memset(wsT_f[64:128, :, 0:64], 0.0), r=(), w=(tk(0),))
            cp(wsT_bf[li][:], wsT_f, r=(tk(0),), w=(("wsT_bf", li),), eng="dve")
            b = next_ps()
            cp(B16[:, 0, :], T[:, 0, :], r=(tk(0),), w=(bk(0),), eng="dve")
            tt(B16[:, 1, :], T[:, 0, :], B16[:, 0, :], ALU.subtract, r=(tk(0), bk(0)), w=(bk(1),))
            mm(ps[:, b, :], ones_b[:], B16[:, 0, :], True, False, r=("ones_b", bk(0)), w=(psk(b),))
            mm(ps[:, b, :], ones_b[:], B16[:, 1, :], False, True, r=("ones_b", bk(1)), w=(psk(b),))
            for cc in range(8):
                g = cc // 2
                stt(Bt[li][:, cc, :], ps[:, b, g * 128:(g + 1) * 128], vecs[li][:, V_LNB + cc:V_LNB + cc + 1],
                    bs_bc[:, g, :], ALU.mult, ALU.add, r=(psk(b), ("vecs", li), tk(1)), w=(("Bt", li),))
            lam = vecs[li][:, V_LAM:V_LAM + 10]
            act(nca[li][:, 0:10], lam, AF.Exp, r=(("vecs", li),), w=(("nca", li),), scale=-1.0)
            act(nca[li][:, 0:10], nca[li][:, 0:10], AF.Ln, r=(("nca", li),), w=(("nca", li),), bias=1.0)
            ts(nca[li][:, 10:20], nca[li][:, 0:10], -8.0, None, ALU.mult, None, r=(("nca", li),),
               w=(("nca", li),))
            ts(nca[li][:, 0:10], nca[li][:, 0:10], -4.0, None, ALU.mult, None, r=(("nca", li),), w=(("nca", li),))
            ts(nca[li][:, 20:30], vecs[li][:, V_BA:V_BA + 10], 0.5, None, ALU.mult, None,
               r=(("vecs", li),), w=(("nca", li),))
            ts(nca[li][:, 30:40], vecs[li][:, V_BX:V_BX + 10], 0.5, None, ALU.mult, None,
               r=(("vecs", li),), w=(("nca", li),))

        x_view = x_d.rearrange("(t p) d -> p t d", p=128)
        out_view = out_d.rearrange("(t p) d -> p t d", p=128)

        for c in range(nch):
            c0 = c * NT
            S.epoch += 1
            if c > 0:
                S.dma("sp", (lambda e, c0=c0: e.dma_start(out=cs_sb[:, 0, :], in_=cos_d[:, c0:c0 + NT])),
                      w=("cs",), key="cs")
                S.dma("sp", (lambda e, c0=c0: e.dma_start(out=cs_sb[:, 1, :], in_=sin_d[:, c0:c0 + NT])),
                      w=("cs",), key="cs")
            for li, l in enumerate(layers):
                if li > 0:
                    S.epoch += 1
                VK = ("vecs", li)
                vv = vecs[li]
                st["tlim"] = NTB
                S.tag = 'S0.c%d.l%d' % (c, li)
                sc = next_sm(TPC)
                SMK = tuple(("sm", sc + q_) for q_ in range(TPC))
                for t_ in range(TPC):
                    act(htok[:, t_, :], x_sb[:, t_, :], AF.Square, r=(("x", t_),), w=(("htok", t_), SMK[t_]),
                        accum=small[:, sc + t_:sc + t_ + 1])
                rsqrt_small(small[:, sc:sc + TPC], small[:, sc:sc + TPC], r=SMK, w=SMK, scale=1.0 / D)
                for t_ in range(TPC):
                    xt = x_sb[:, t_, :]
                    ssq = small[:, sc + t_:sc + t_ + 1]
                    stt(htok[:, t_, :], xt, ssq, gbc[:], ALU.mult, ALU.mult, r=(("x", t_), SMK[t_], "gbc"),
                        w=(("htok", t_),))
                    b = next_ps()
                    psb = ps[:, b, :].bitcast(BF16)
                    for kc in range(8):
                        S.add("pe", (lambda e, t_=t_, kc=kc, psb=psb: e.transpose(
                            psb[:, kc * 128:(kc + 1) * 128], htok[:, t_, kc * 128:(kc + 1) * 128], ident[:])),
                            r=(("htok", t_), "ident"), w=(psk(b),))
                    cp(hT[:, :, t_ * 128:(t_ + 1) * 128], psb.rearrange("p (k t) -> p k t", k=8),
                       r=(psk(b),), w=("hT",), eng=("dve" if t_ % 2 else "act"))

                if not (c == nch - 1 and li == NL - 1):
                    l_next = layers[(li + 1) % NL]
                    S.dma("sp", (lambda e, l_next=l_next: e.dma_start(
                        out=gbc[:], in_=gpre_d[l_next].partition_broadcast(128))), w=("gbc",), key="gbc")
                if stop <= 0:
                    continue
                S.tag = 'S1.c%d.l%d' % (c, li)
                wk1, (w_cq,) = wnext("cq")
                wk2, (w_ckv, w_krs) = wnext("ckv")

                def ln_proj(wkey, wv, nun):
                    tb = [next_t() for _ in range(nun)]
                    tq = []
                    for m in range(nun):
                        b = next_ps()
                        for kc in range(8):
                            mm(ps[:, b, :], wv[:, kc, m * 128:(m + 1) * 128], hT[:, kc, :],
                               kc == 0, kc == 7, r=(wkey, "hT"), w=(psk(b),))
                        q_ = next_t()
                        tq.append(q_)
                        act(T[:, q_, :], ps[:, b, :], AF.Square, r=(psk(b),), w=(tk(q_),))
                        cp(T[:, tb[m], :], ps[:, b, :], r=(psk(b),), w=(tk(tb[m]),), eng="dve")
                    return tb, tq

                def ln_hilo(tq):
                    hl = []
                    for q_ in tq:
                        bh, bl = next_b(), next_b()
                        cp(B16[:, bh, :], T[:, q_, :], r=(tk(q_),), w=(bk(bh),), eng="dve")
                        tt(B16[:, bl, :], T[:, q_, :], B16[:, bh, :], ALU.subtract, r=(tk(q_), bk(bh)),
                           w=(bk(bl),))
                        hl.append((bh, bl))
                    return hl

                def ln_ones(hl):
                    bss = next_ps()
                    n_ = len(hl)
                    for m, (bh, bl) in enumerate(hl):
                        mm(ps[:, bss, :], ones_b[:], B16[:, bh, :], m == 0, False, r=("ones_b", bk(bh)),
                           w=(psk(bss),))
                        mm(ps[:, bss, :], ones_b[:], B16[:, bl, :], False, m == n_ - 1, r=("ones_b", bk(bl)),
                           w=(psk(bss),))
                    return bss

                def ln_fin(bss, tb, tq, gcol, dst_fn, dst_key):
                    nun = len(tb)
                    tr = tq[0]
                    act(T[:, tr, :], ps[:, bss, :], AF.Ln, r=(psk(bss),), w=(tk(tr),), bias=EPS,
                        scale=1.0 / (128 * nun))
                    act(T[:, tr, :], T[:, tr, :], AF.Exp, r=(tk(tr),), w=(tk(tr),), scale=-0.5)
                    for m in range(nun):
                        stt(dst_fn(m), T[:, tb[m], :], vv[:, gcol + m:gcol + m + 1], T[:, tr, :], ALU.mult,
                            ALU.mult, r=(tk(tb[m]), tk(tr), VK), w=(dst_key,))

                st["t"] = 0
                tb_q, tq_q = ln_proj(wk1, w_cq, 3)
                hl_q = ln_hilo(tq_q)
                tb_k, tq_k = ln_proj(wk2, w_ckv, 2)
                bss_q = ln_ones(hl_q)
                hl_k = ln_hilo(tq_k)
                bA = next_ps()
                bB = next_ps()
                for kc in range(8):
                    mm(ps[0:64, bA, :], w_ckv[:, kc, 256:320], hT[:, kc, :], kc == 0, kc == 7, r=(wk2, "hT"),
                       w=(psk(bA),))
                for kc in range(8):
                    mm(ps[0:64, bB, :], w_krs[:, kc, :], hT[:, kc, :], kc == 0, kc == 7, r=(wk2, "hT"),
                       w=(psk(bB),))
                bss_k = ln_ones(hl_k)
                ln_fin(bss_q, tb_q, tq_q, V_QG, lambda m: cqn[:, m, :], "cqn")
                ln_fin(bss_k, tb_k, tq_k, V_KVG, lambda m: ckv_c[li][:, m, c0:c0 + NT], ("ckv", li))
                t1 = next_t()
                t2 = next_t()
                tt(T[0:64, t1, :], ps[0:64, bA, :], cs_sb[:, 0, :], ALU.mult, r=(psk(bA), "cs"), w=(tk(t1),))
                tt(T[0:64, t2, :], ps[0:64, bB, :], cs_sb[:, 1, :], ALU.mult, r=(psk(bB), "cs"), w=(tk(t2),))
                tt(kr_c[li][0:64, c0:c0 + NT], T[0:64, t1, :], T[0:64, t2, :], ALU.add, r=(tk(t1), tk(t2)),
                   w=(("kr", li),))

                if stop <= 1:
                    continue
                S.tag = 'Av.c%d.l%d' % (c, li)
                wv_k = []
                wv_v = []
                for n in range(2):
                    k_, (v_,) = wnext("wv%d" % n)
                    wv_k.append(k_)
                    wv_v.append(v_)
                for t_ in range(TPC):
                    b = next_ps(2)
                    for n in range(2):
                        for kc in range(8):
                            mm(ps[:, b + n, :], hT[:, kc, t_ * 128:(t_ + 1) * 128], wv_v[n][:, kc, :], kc == 0,
                               kc == 7, r=(wv_k[n], "hT"), w=(psk(b + n),))
                        S.add("dve", (lambda e, n=n, b=b: e.bn_stats(out=bnst[:, n, :], in_=ps[:, b + n, :])),
                              r=(psk(b + n),), w=("bnst",))
                    sc = next_sm(2)
                    S.add("dve", (lambda e, sc=sc: e.bn_aggr(out=small[:, sc:sc + 2],
                                                             in_=bnst[:].rearrange("p a b -> p (a b)"))),
                          r=("bnst",), w=(("sm", sc), ("sm", sc + 1)))
                    rsqrt_small(small[:, sc + 1:sc + 2], small[:, sc + 1:sc + 2], r=(("sm", sc + 1),),
                                w=(("sm", sc + 1),), scale=1.0)
                    ts(htok[:, t_, :].rearrange("p (n f) -> p n f", n=2), ps[:, b:b + 2, :], small[:, sc:sc + 1],
                       small[:, sc + 1:sc + 2], ALU.subtract, ALU.mult,
                       r=(psk(b), psk(b + 1), ("sm", sc), ("sm", sc + 1)), w=(("htok", t_),))
                S.tag = 'S3.c%d.l%d' % (c, li)
                nk = TPC * (c + 1)
                st["pslim"] = 4
                st["ps"] = st["ps"] % 4

                def head_finalize(h_, tz_, bo_, bs_):
                    tr = next_t()
                    S.add("dve", (lambda e, tr=tr, bs_=bs_: e.reciprocal(out=T[:, tr, :], in_=ps[:, bs_, :])),
                          r=(psk(bs_),), w=(tk(tr),))
                    stt(T[:, tr, :], ps[:, bo_, :], 0.5, T[:, tr, :], ALU.mult, ALU.mult, r=(psk(bo_), tk(tr)),
                        w=(tk(tr),))
                    tt(y_b[:, h_, :], T[:, tr, :], T[:, tz_, :], ALU.mult, r=(tk(tr), tk(tz_)), w=("y_b",),
                       eng="dve")

                pending = None
                for h in range(H):
                    q0c = h * 192
                    bo, bsum = (4, 5) if h % 2 == 0 else (6, 7)
                    wkh, (wq, wqs, wkv, wzb) = wnext("h%d" % h)
                    b = next_ps()
                    for kc in range(3):
                        mm(ps[:, b, :], wq[:, kc, 0:128], cqn[:, kc, :], kc == 0, kc == 2, r=(wkh, "cqn"),
                           w=(psk(b),))
                    qp = h % 2
                    QN = ("qn", qp)
                    QR = ("qr", qp)
                    cp(qbuf[:, qp, 0, :], ps[:, b, :], r=(psk(b),), w=(QN,), eng="act")
                    bA = next_ps()
                    bB = next_ps()
                    for kc in range(3):
                        mm(ps[0:64, bA, :], wq[:, kc, 128:192], cqn[:, kc, :], kc == 0, kc == 2, r=(wkh, "cqn"),
                           w=(psk(bA),))
                    for kc in range(3):
                        mm(ps[0:64, bB, :], wqs[:, kc, :], cqn[:, kc, :], kc == 0, kc == 2, r=(wkh, "cqn"),
                           w=(psk(bB),))
                    t1 = next_t()
                    t2 = next_t()
                    tt(T[0:64, t1, :], ps[0:64, bA, :], cs_sb[:, 0, :], ALU.mult, r=(psk(bA), "cs"), w=(tk(t1),))
                    tt(T[0:64, t2, :], ps[0:64, bB, :], cs_sb[:, 1, :], ALU.mult, r=(psk(bB), "cs"), w=(tk(t2),))
                    tt(qbuf[0:64, qp, 1, :], T[0:64, t1, :], T[0:64, t2, :], ALU.add, r=(tk(t1), tk(t2)),
                       w=(QR,))
                    for kb in range(c + 1):
                        b = next_ps()
                        for kc in range(2):
                            mm(ps[:, b, :], wkv[:, kc, 0:128], ckv_c[li][:, kc, kb * NT:(kb + 1) * NT], kc == 0,
                               kc == 1, r=(wkh, ("ckv", li)), w=(psk(b),))
                        cp(k_h[:, kb * NT:(kb + 1) * NT], ps[:, b, :], r=(psk(b),), w=("k_h",),
                           eng=("act" if kb % 2 else "dve"))
                        b = next_ps()
                        for j in range(4):
                            for kc in range(2):
                                mm(ps[:, b, j * 128:(j + 1) * 128],
                                   ckv_c[li][:, kc, (4 * kb + j) * 128:(4 * kb + j + 1) * 128],
                                   wkv[:, kc, 128:256], kc == 0, kc == 1, r=(wkh, ("ckv", li)), w=(psk(b),))
                        cp(v_h[:, 4 * kb:4 * kb + 4, :], ps[:, b, :].rearrange("p (j d) -> p j d", j=4),
                           r=(psk(b),), w=("v_h",), eng=("dve" if kb % 2 else "act"))
                    bz = next_ps()
                    for kc in range(8):
                        mm(ps[:, bz, :], wzb[:, kc, :], hT[:, kc, :], kc == 0, kc == 7, r=(wkh, "hT"), w=(psk(bz),))
                    tz = next_t()
                    tanh_half(T[:, tz, :], ps[:, bz, :], r_src=(psk(bz),), w_dst=(tk(tz),))
                    stt(T[:, tz, :], T[:, tz, :], 1.0, ps[:, bz, :], ALU.add, ALU.mult, r=(psk(bz), tk(tz)),
                        w=(tk(tz),))
                    if pending is not None:
                        head_finalize(*pending)
                    LOOK = 3
                    sbank = {}

                    def scores(kt):
                        j = kt - TPC * c
                        q0 = 128 * max(j, 0)
                        b = next_ps()
                        sbank[kt] = b
                        mm(ps[:, b, q0:NT], k_h[:, kt * 128:(kt + 1) * 128], qbuf[:, qp, 0, q0:NT], True, False,
                           r=("k_h", QN), w=(psk(b),))
                        mm(ps[:, b, q0:NT], kr_c[li][:, kt * 128:(kt + 1) * 128], qbuf[:, qp, 1, q0:NT], False,
                           True, r=(("kr", li), QR), w=(psk(b),))

                    for kt in range(min(LOOK, nk)):
                        scores(kt)
                    for kt in range(nk):
                        if kt + LOOK < nk:
                            scores(kt + LOOK)
                        j = kt - TPC * c
                        q0 = 128 * max(j, 0)
                        b = sbank.pop(kt)
                        pt = next_b()
                        if j < 0:
                            act(B16[:, pt, :], ps[:, b, :], AF.Exp, r=(psk(b),), w=(bk(pt),), scale=SCALE)
                        else:
                            act(B16[:, pt, q0 + 64:NT], ps[:, b, q0 + 64:NT], AF.Exp, r=(psk(b),), w=(bk(pt),),
                                scale=SCALE)
                            act(B16[0:64, pt, q0:q0 + 64], ps[0:64, b, q0:q0 + 64], AF.Exp, r=(psk(b),),
                                w=(bk(pt),), scale=SCALE)
                            S.add("dve", (lambda e, pt=pt, q0=q0: e.memset(B16[64:128, pt, q0:q0 + 64], 0.0)),
                                  r=(), w=(bk(pt),))
                        mm(ps[:, bo, q0:NT], v_h[:, kt, :], B16[:, pt, q0:NT], kt == 0, kt == nk - 1,
                           r=("v_h", bk(pt)), w=(psk(bo),))
                        mm(ps[:, bsum, q0:NT], ones_b[:], B16[:, pt, q0:NT], kt == 0, kt == nk - 1,
                           r=("ones_b", bk(pt)), w=(psk(bsum),))
                    pending = (h, tz, bo, bsum)
                head_finalize(*pending)

                if stop <= 2:
                    continue
                S.tag = 'A.c%d.l%d' % (c, li)
                st["pslim"] = 8
                wu = {}
                for cc in range(8):
                    if cc % 4 == 0:
                        ku, (vu,) = wnext("u%d" % (cc // 4))
                        kz, (vz,) = wnext("z%d" % (cc // 4))
                    g = cc // 2
                    bs_ = next_ps()
                    for t_ in range(TPC):
                        mm(ps[:, bs_, t_ * 128:(t_ + 1) * 128], htok[:, t_, cc * 128:(cc + 1) * 128],
                           wsT_bf[li][:, g, :], True, True, r=(("htok", t_), ("wsT_bf", li)), w=(psk(bs_),))
                    bu = next_ps()
                    for kc in range(8):
                        mm(ps[:, bu, :], vu[:, kc, (cc % 4) * 128:(cc % 4 + 1) * 128], hT[:, kc, :], kc == 0, kc == 7,
                           r=(ku, "hT"), w=(psk(bu),))
                    bz = next_ps()
                    for kc in range(8):
                        mm(ps[:, bz, :], vz[:, kc, (cc % 4) * 128:(cc % 4 + 1) * 128], hT[:, kc, :], kc == 0, kc == 7,
                           r=(kz, "hT"), w=(psk(bz),))
                    ta = next_t()
                    stt(T[:, ta, :].rearrange("p (t i) -> p t i", t=4),
                        ps[:, bs_, :].rearrange("p (t i) -> p t i", t=4),
                        vv[:, V_LNG + cc:V_LNG + cc + 1],
                        Bt[li][:, cc, :].unsqueeze(1).broadcast_to([128, 4, 128]),
                        ALU.mult, ALU.add, r=(psk(bs_), VK, ("Bt", li)), w=(tk(ta),))
                    stt(T[:, ta, :], ps[:, bu, :], 0.5, T[:, ta, :], ALU.mult, ALU.mult, r=(psk(bu), tk(ta)),
                        w=(tk(ta),))
                    tz = next_t()
                    tanh_half(T[:, tz, :], ps[:, bz, :], r_src=(psk(bz),), w_dst=(tk(tz),))
                    stt(T[:, tz, :], T[:, tz, :], 1.0, ps[:, bz, :], ALU.add, ALU.mult, r=(psk(bz), tk(tz)),
                        w=(tk(tz),))
                    tt(y_a[:, cc, :], T[:, ta, :], T[:, tz, :], ALU.mult, r=(tk(ta), tk(tz)), w=("y_a",),
                       eng="dve")

                if stop <= 3:
                    continue
                S.tag = 'C.c%d.l%d' % (c, li)
                for hf in range(2):
                    for uu in range(5):
                        u = hf * 5 + uu
                        if uu == 0:
                            kx, (vx,) = wnext("xc%da" % hf)
                        elif uu == 3:
                            kx, (vx,) = wnext("xc%db" % hf)
                        uo = uu if uu < 3 else uu - 3
                        b = next_ps()
                        for kc in range(8):
                            mm(ps[:, b, :], vx[:, kc, uo * 128:(uo + 1) * 128], hT[:, kc, :], kc == 0, kc == 7,
                               r=(kx, "hT"), w=(psk(b),))
                        rb = uu % 2
                        RK = ("raw", rb)
                        cp(raw[:, rb, 4:NT + 4], ps[:, b, :], r=(psk(b),), w=(RK,), eng="act")
                        cp(raw[:, rb, 1:4], hist[li][:, u, 1:4], r=(("hist", li),), w=(RK,), eng="dve")
                        cw = V_CW + u * 4
                        act(xcf[:, uu, :], ps[:, b, :], AF.Identity, r=(psk(b), VK), w=(("xcf", uu),),
                            scale=vv[:, cw + 3:cw + 4], bias=vv[:, V_CB + u:V_CB + u + 1])
                        for k in range(3):
                            stt(xcf[:, uu, :], raw[:, rb, 1 + k:1 + k + NT], vv[:, cw + k:cw + k + 1], xcf[:, uu, :],
                                ALU.mult, ALU.add, r=(RK, VK, ("xcf", uu)), w=(("xcf", uu),))
                        cp(hist[li][:, u, 1:4], raw[:, rb, NT + 1:NT + 4], r=(RK,), w=(("hist", li),), eng="dve")
                        cp(xcb[:, uu, :], xcf[:, uu, :], r=(("xcf", uu),), w=(("xcb", uu),), eng="dve")
                    kg, (vga, vgx) = wnext("g%d" % hf)
                    NK = ("nca", li)
                    n = 5
                    st["t"] = 0
                    tA, tI = next_t(n), next_t(n)
                    t_r = next_t()
                    for uu in range(n):
                        u = hf * 5 + uu
                        prs = [(qi, ki) for qi, (ou, ki) in enumerate(GATE_PAIRS) if ou == uu]
                        br = next_ps()
                        for n_, (qi, ki) in enumerate(prs):
                            mm(ps[:, br, :], vga[:, qi, :], xcb[:, ki, :], n_ == 0, n_ == len(prs) - 1,
                               r=(kg, ("xcb", ki)), w=(psk(br),))
                        bi = next_ps()
                        for n_, (qi, ki) in enumerate(prs):
                            mm(ps[:, bi, :], vgx[:, qi, :], xcb[:, ki, :], n_ == 0, n_ == len(prs) - 1,
                               r=(kg, ("xcb", ki)), w=(psk(bi),))
                        tanh_half(T[:, t_r, :], ps[:, br, :], r_src=(psk(br), NK), w_dst=(tk(t_r),),
                                  bias=nca[li][:, 20 + u:21 + u])
                        tanh_half(T[:, tI + uu, :], ps[:, bi, :], r_src=(psk(bi), NK), w_dst=(tk(tI + uu),),
                                  bias=nca[li][:, 30 + u:31 + u])
                        act(T[:, tA + uu, :], T[:, t_r, :], AF.Exp, r=(tk(t_r), NK), w=(tk(tA + uu),),
                            scale=nca[li][:, u:u + 1], bias=nca[li][:, u:u + 1])
                    bzs = []
                    for uu in range(n):
                        if uu == 0:
                            kzc, (vzc,) = wnext("zc%da" % hf)
                        elif uu == 3:
                            kzc, (vzc,) = wnext("zc%db" % hf)
                        uo = uu if uu < 3 else uu - 3
                        bz = next_ps()
                        bzs.append(bz)
                        for kc in range(8):
                            mm(ps[:, bz, :], vzc[:, kc, uo * 128:(uo + 1) * 128], hT[:, kc, :], kc == 0, kc == 7,
                               r=(kzc, "hT"), w=(psk(bz),))
                    AK = tuple(tk(tA + i_) for i_ in range(n))
                    IK = tuple(tk(tI + i_) for i_ in range(n))
                    XK = tuple(("xcf", uu) for uu in range(n))
                    Av_ = T[:, tA:tA + n, :]
                    Iv = T[:, tI:tI + n, :]
                    Xv = xcf[:, 0:n, :]
                    stt(Xv, Iv, 1.0, Xv, ALU.add, ALU.mult, r=IK + XK, w=XK)
                    act(Iv, Av_, AF.Square, r=AK, w=IK)
                    ts(Iv, Iv, 0.9999999, -1.0, ALU.min, ALU.mult, r=IK, w=IK)
                    act(Iv, Iv, AF.Ln, r=IK, w=IK, bias=1.0)
                    act(Iv, Iv, AF.Exp, r=IK + ("lnq",), w=IK, scale=0.5, bias=LN_QUARTER[:, 0:1])
                    tt(Iv, Iv, Xv, ALU.mult, r=IK + XK, w=IK)
                    for uu in range(n):
                        u = hf * 5 + uu
                        HK = ("hst", li, u)
                        S.add("dve", (lambda e, uu=uu, a_=tA + uu, m_=tI + uu, li=li, u=u:
                                      e.tensor_tensor_scan(out=xcf[:, uu, :], data0=T[:, a_, :], data1=T[:, m_, :],
                                                           initial=hst[li][:, u:u + 1], op0=ALU.mult,
                                                           op1=ALU.add)),
                              r=(tk(tA + uu), tk(tI + uu), HK), w=(("xcf", uu),))
                        cp(hst[li][:, u:u + 1], xcf[:, uu, NT - 1:NT], r=(("xcf", uu),), w=(HK,), eng="dve")
                    for uu in range(n):
                        u = hf * 5 + uu
                        bz = bzs[uu]
                        tz = tA + uu
                        tanh_half(T[:, tz, :], ps[:, bz, :], r_src=(psk(bz),), w_dst=(tk(tz),))
                        stt(T[:, tz, :], T[:, tz, :], 1.0, ps[:, bz, :], ALU.add, ALU.mult, r=(psk(bz), tk(tz)),
                            w=(tk(tz),))
                        tt(y_c[:, u, :], xcf[:, uu, :], T[:, tz, :], ALU.mult, r=(("xcf", uu), tk(tz)),
                           w=("y_c",), eng="pool")

                if stop <= 4:
                    continue
                S.tag = 'M.c%d.l%d' % (c, li)
                st["tlim"] = NTB - 2
                st["t"] = 0
                S.dma("sp", (lambda e, l=l: e.dma_start(out=T[:, NTB - 2:NTB, :].rearrange("p a b -> p (a b)"),
                                                        in_=gpost_d[l].partition_broadcast(128))),
                      w=(tk(NTB - 2), tk(NTB - 1)), key="gpo")
                for dc in range(8):
                    kgt, vg = wnext("G%d" % dc)
                    kpr, (vpa, vpb, vpc) = wnext("P%d" % dc)
                    t_acc = next_t()
                    mixers = [(y_a, "y_a", 8, vpa), (y_b, "y_b", 8, vpb), (y_c, "y_c", 10, vpc)]
                    bgs, t_ss = [], []
                    for m in range(3):
                        bg = next_ps()
                        for kc in range(8):
                            mm(ps[:, bg, :], vg[m][:, kc, :], hT[:, kc, :], kc == 0, kc == 7, r=(kgt, "hT"),
                               w=(psk(bg),))
                        t_s = next_t()
                        tanh_half(T[:, t_s, :], ps[:, bg, :], r_src=(psk(bg),), w_dst=(tk(t_s),))
                        bgs.append(bg)
                        t_ss.append(t_s)
                    MGK = ("htok", dc // 2)
                    for m, (yb, ykey, nkc, wp) in enumerate(mixers):
                        bp = next_ps()
                        for k in range(nkc):
                            mm(ps[:, bp, :], wp[:, k, :], yb[:, k, :], k == 0, k == nkc - 1, r=(kpr, ykey),
                               w=(psk(bp),))
                        t_s = t_ss[m]
                        if m == 0:
                            stt(T[:, t_acc, :], T[:, t_s, :], 1.0, ps[:, bp, :], ALU.add, ALU.mult,
                                r=(psk(bp), tk(t_s)), w=(tk(t_acc),))
                        else:
                            stt(T[:, t_s, :], T[:, t_s, :], 1.0, ps[:, bp, :], ALU.add, ALU.mult,
                                r=(psk(bp), tk(t_s)), w=(tk(t_s),))
                            if m == 1:
                                tt(T[:, t_acc, :], T[:, t_acc, :], T[:, t_s, :], ALU.add, r=(tk(t_acc), tk(t_s)),
                                   w=(tk(t_acc),), eng="dve")
                            else:
                                tt(mrg[:, dc, :], T[:, t_acc, :], T[:, t_s, :], ALU.add, r=(tk(t_acc), tk(t_s)),
                                   w=(MGK,), eng="dve")

                if stop <= 5:
                    continue
                S.tag = 'O.c%d.l%d' % (c, li)
                wo_k = []
                wo_v = []
                for n in range(2):
                    k_, (v_,) = wnext("wo%d" % n)
                    wo_k.append(k_)
                    wo_v.append(v_)
                for t_ in range(TPC):
                    b = next_ps(2)
                    for n in range(2):
                        for kc in range(8):
                            mm(ps[:, b + n, :], mrg[:, kc, t_ * 128:(t_ + 1) * 128], wo_v[n][:, kc, :], kc == 0,
                               kc == 7, r=(wo_k[n],) + tuple(("htok", q_) for q_ in range(4)), w=(psk(b + n),))
                    sc = next_sm()
                    ssq = small[:, sc:sc + 1]
                    act(y_a[:, 2 * t_:2 * t_ + 2, :], ps[:, b:b + 2, :], AF.Square,
                        r=(psk(b), psk(b + 1)), w=("y_a", ("sm", sc)), accum=ssq)
                    rsqrt_small(ssq, ssq, r=(("sm", sc),), w=(("sm", sc),), scale=1.0 / D, eps=4.0 * EPS)
                    t2 = next_t(2)
                    stt(T[:, t2:t2 + 2, :], ps[:, b:b + 2, :], ssq, T[:, NTB - 2:NTB, :],
                        ALU.mult, ALU.mult, r=(psk(b), psk(b + 1), ("sm", sc), tk(NTB - 2), tk(NTB - 1)),
                        w=(tk(t2), tk(t2 + 1)))
                    tt(x_sb[:, t_, :].rearrange("p (n f) -> p n f", n=2),
                       x_sb[:, t_, :].rearrange("p (n f) -> p n f", n=2), T[:, t2:t2 + 2, :], ALU.add,
                       r=(("x", t_), tk(t2), tk(t2 + 1)), w=(("x", t_),), eng="dve")
                    if li == NL - 1:
                        S.dma("sp", (lambda e, c=c, t_=t_: e.dma_start(out=out_view[:, c * TPC + t_, :],
                                                                        in_=x_sb[:, t_, :])),
                              r=(("x", t_),), w=(("outd", t_),), key=("xo", t_))
                        if c + 1 < nch:
                            S.dma("pool", (lambda e, c=c, t_=t_: e.dma_start(out=x_sb[:, t_, :],
                                                                              in_=x_view[:, (c + 1) * TPC + t_, :])),
                                  w=(("x", t_),), key=("xl", t_))

        n_out = {t_: S.dma_cnt.get(("xo", t_), 0) for t_ in range(TPC)}

        sig, rank = S.analyse()
        nep = S.n_epochs()
        eng_sems = {e: [es.enter_context(nc.semaphore(f"s_{e}_{i}")) for i in range(nep)] for e in Sched.ENGS}
        dma_sems = {k: es.enter_context(nc.semaphore(f"d_{i}")) for i, k in enumerate(S.dma_cnt.keys())}
        block = es.enter_context(nc.Block())

        @block.tensor
        def _(e):
            S.emit("pe", e, eng_sems, dma_sems, sig, rank)

        @block.scalar
        def _(e):
            S.emit("act", e, eng_sems, dma_sems, sig, rank)

        @block.vector
        def _(e):
            S.emit("dve", e, eng_sems, dma_sems, sig, rank)

        @block.gpsimd
        def _(e):
            S.emit("pool", e, eng_sems, dma_sems, sig, rank)

        @block.sync
        def _(e):
            S.emit("sp", e, eng_sems, dma_sems, sig, rank)
            for t_ in range(TPC):
                if n_out[t_]:
                    e.wait_ge(dma_sems[("xo", t_)], 16 * n_out[t_])

    nc._sched = S
    return nc


_PROG_CACHE = {}


def _host_layout(inp):
    L = 2
    f32 = np.float32
    def pad_gate(w):
        full = np.zeros((L, 1280, 1280), f32)
        for hb in range(16):
            full[:, hb * 80:(hb + 1) * 80, hb * 80:(hb + 1) * 80] = w[:, hb]
        tiles = np.zeros((L, 26, 128, 128), f32)
        for hf in range(2):
            for qi, (ou, ki) in enumerate(GATE_PAIRS):
                r0 = (hf * 5 + ki) * 128
                c0 = (hf * 5 + ou) * 128
                tiles[:, hf * 13 + qi] = full[:, r0:r0 + 128, c0:c0 + 128]
        return tiles

    vecs = np.zeros((L, 128, NV), f32)

    def chunks(v, n):
        return np.ascontiguousarray(v.reshape(L, n, 128).transpose(0, 2, 1))

    vecs[:, :, V_LNG:V_LNG + 8] = chunks(inp["gm_ln_g"], 8)
    vecs[:, :, V_LNB:V_LNB + 8] = chunks(inp["gm_ln_b"], 8)
    vecs[:, :, V_QG:V_QG + 3] = chunks(inp["mla_q_norm_g"], 3)
    vecs[:, :, V_KVG:V_KVG + 2] = chunks(inp["mla_kv_norm_g"], 2)
    cw = inp["lru_conv_w"].reshape(L, 4, 10, 128).transpose(0, 3, 2, 1)
    vecs[:, :, V_CW:V_CW + 40] = cw.reshape(L, 128, 40)
    vecs[:, :, V_CB:V_CB + 10] = chunks(inp["lru_conv_b"], 10)
    vecs[:, :, V_BA:V_BA + 10] = chunks(inp["lru_b_a"], 10)
    vecs[:, :, V_BX:V_BX + 10] = chunks(inp["lru_b_x"], 10)
    vecs[:, :, V_LAM:V_LAM + 10] = chunks(inp["lru_lambda"], 10)

    pos = np.arange(SEQ, dtype=f32)
    inv_freq = (10000.0 ** (-np.arange(0, 64, 2, dtype=f32) / f32(64))).astype(f32)
    ang = (pos[:, None] * inv_freq[None, :]).astype(f32)
    cos = np.cos(ang).astype(f32).T
    sin = np.sin(ang).astype(f32).T
    arrs = dict(
        w_in=np.asarray(inp["w_in"], f32),
        w_uq=np.asarray(inp["mla_w_uq"], f32),
        w_ukv=np.asarray(inp["mla_w_ukv"], f32),
        w_pa=np.asarray(inp["w_proj_a"], f32),
        w_pb=np.asarray(inp["w_proj_b"], f32),
        w_pc=np.asarray(inp["w_proj_c"], f32),
        w_out=np.asarray(inp["w_out"], f32),
        wa_pad=pad_gate(np.asarray(inp["lru_w_a"], f32)),
        wx_pad=pad_gate(np.asarray(inp["lru_w_x"], f32)),
    )
    shared = dict(
        wimg=build_slot_images(arrs),
        wsT=np.ascontiguousarray(np.asarray(inp["gm_ws"], f32).transpose(0, 3, 1, 2)),
        bs=np.ascontiguousarray(np.asarray(inp["gm_bs"], f32).reshape(L, 512)),
        vecs=vecs,
        gpre=np.ascontiguousarray(inp["pre_norm_g"], f32),
        gpost=np.ascontiguousarray(inp["post_norm_g"], f32),
        cos2=np.ascontiguousarray(np.concatenate([cos, cos], 0)),
        sin2=np.ascontiguousarray(np.concatenate([-sin, sin], 0)),
        ident=np.eye(128, dtype=f32),
    )
    return shared


def _run(layers, x, shared):
    key = tuple(layers)
    if key not in _PROG_CACHE:
        _PROG_CACHE[key] = build_program(list(layers))
    nc = _PROG_CACHE[key]
    in_maps = []
    for b in range(8):
        m = dict(shared)
        m["x"] = np.ascontiguousarray(x[b], np.float32)
        in_maps.append(m)
    res = run_bass_kernel_spmd(nc, in_maps, core_ids=list(range(8)))
    return np.stack([np.asarray(r["out"], np.float32) for r in res.results], 0)


FUSED = True


def kernel(**inputs):
    inp = {k: np.asarray(v) for k, v in inputs.items()}
    shared = _host_layout(inp)
    x = np.asarray(inp["x"], np.float32)
    if FUSED:
        return _run((0, 1), x, shared)
    y = _run((0,), x, shared)
    return _run((1,), y, shared)
```

```python
import math
from contextlib import ExitStack

import numpy as np
import concourse.bass as bass
import concourse.mybir as mybir
from concourse.bass_utils import run_bass_kernel_spmd

F32 = mybir.dt.float32
BF16 = mybir.dt.bfloat16
ALU = mybir.AluOpType
AF = mybir.ActivationFunctionType

D = 1024
SEQ = 2048
NT = 512
NCH = SEQ // NT
TPC = NT // 128
EPS = 1e-6
H = 8
N_IN = 10432
O_U, O_V, O_ZA, O_CQ, O_CKV, O_KR, O_ZB, O_XC, O_ZC, O_GA, O_GB, O_GC = (
    0, 1024, 2048, 3072, 3456, 3712, 3776, 4800, 6080, 7360, 8384, 9408)
SCALE = 1.0 / math.sqrt(192.0)
V_LNG, V_LNB, V_QG, V_KVG, V_CW, V_CB, V_BA, V_BX, V_LAM = 0, 8, 16, 19, 21, 61, 71, 81, 91
NV = 101
GATE_PAIRS = [(0, 0), (0, 1), (1, 0), (1, 1), (1, 2), (2, 1), (2, 2), (2, 3), (3, 2), (3, 3), (3, 4),
              (4, 3), (4, 4)]
WSLOT = 4096
NSLOT = 4
LOOKAHEAD = 2
NTB = 12
NB16 = 8


def req_plan():
    def cols(arr, c0, c1):
        return ("cols", arr, c0, c1)
    P = []
    P.append(("cq", [(0, [8, 384], cols("w_in", O_CQ, O_CQ + 384))]))
    P.append(("ckv", [(0, [8, 320], cols("w_in", O_CKV, O_CKV + 320)), (8 * 320, [8, 64], ("ropesw", "w_in", O_KR))]))
    for n in range(2):
        P.append(("wv%d" % n, [(0, [8, 512], cols("w_in", O_V + n * 512, O_V + (n + 1) * 512))]))
    for h in range(H):
        q0c = h * 192
        P.append(("h%d" % h, [
            (0, [3, 192], cols("w_uq", q0c, q0c + 192)),
            (576, [3, 64], ("ropesw", "w_uq", q0c + 128)),
            (768, [2, 256], cols("w_ukv", h * 256, (h + 1) * 256)),
            (1280, [8, 128], cols("w_in", O_ZB + h * 128, O_ZB + (h + 1) * 128))]))
    for n in range(2):
        P.append(("u%d" % n, [(0, [8, 512], cols("w_in", O_U + n * 512, O_U + (n + 1) * 512))]))
        P.append(("z%d" % n, [(0, [8, 512], cols("w_in", O_ZA + n * 512, O_ZA + (n + 1) * 512))]))
    for hf in range(2):
        P.append(("xc%da" % hf, [(0, [8, 384], cols("w_in", O_XC + hf * 640, O_XC + hf * 640 + 384))]))
        P.append(("xc%db" % hf, [(0, [8, 256], cols("w_in", O_XC + hf * 640 + 384, O_XC + (hf + 1) * 640))]))
        P.append(("g%d" % hf, [(0, [13, 128], ("pad", "wa_pad", hf)), (13 * 128, [13, 128], ("pad", "wx_pad", hf))]))
        P.append(("zc%da" % hf, [(0, [8, 384], cols("w_in", O_ZC + hf * 640, O_ZC + hf * 640 + 384))]))
        P.append(("zc%db" % hf, [(0, [8, 256], cols("w_in", O_ZC + hf * 640 + 384, O_ZC + (hf + 1) * 640))]))
    for dc in range(8):
        P.append(("G%d" % dc, [(m_ * 1024, [8, 128], cols("w_in", O_GA + m_ * 1024 + dc * 128,
                                                            O_GA + m_ * 1024 + (dc + 1) * 128)) for m_ in range(3)]))
        P.append(("P%d" % dc, [(0, [8, 128], cols("w_pa", dc * 128, (dc + 1) * 128)),
                               (1024, [8, 128], cols("w_pb", dc * 128, (dc + 1) * 128)),
                               (2048, [10, 128], cols("w_pc", dc * 128, (dc + 1) * 128))]))
    for n in range(2):
        P.append(("wo%d" % n, [(0, [8, 512], cols("w_out", n * 512, (n + 1) * 512))]))
    return P


def req_used(parts):
    return max(off + int(np.prod(shp)) for off, shp, _ in parts)


def build_slot_images(arrs):
    plan = req_plan()
    L = 2
    img = np.zeros((L, len(plan), 128, WSLOT), np.float32)

    def kp(a):
        k = a.shape[0] // 128
        return a.reshape(k, 128, a.shape[1]).transpose(1, 0, 2).reshape(128, -1)

    for l in range(L):
        for j, (name, parts) in enumerate(plan):
            for off, shp, spec in parts:
                n = int(np.prod(shp))
                if spec[0] == "cols":
                    data = kp(arrs[spec[1]][l][:, spec[2]:spec[3]])
                elif spec[0] == "ropesw":
                    a = arrs[spec[1]][l]
                    c0 = spec[2]
                    data = kp(np.concatenate([a[:, c0 + 32:c0 + 64], a[:, c0:c0 + 32]], axis=1))
                else:
                    a = arrs[spec[1]][l, spec[2] * 13:(spec[2] + 1) * 13]
                    data = a.transpose(1, 0, 2).reshape(128, -1)
                assert data.shape == (128, n), (name, data.shape, n)
                img[l, j, :, off:off + n] = data
    return img


class Sched:
    ENGS = ("pe", "act", "dve", "pool", "sp")

    def __init__(self):
        self.ops = {e: [] for e in self.ENGS}
        self.last_w = {}
        self.readers = {}
        self.dma_cnt = {}
        self.epoch = 0
        self.tag = 'setup'

    def _deps(self, reads, writes):
        d = {}

        def put(t):
            k, v = t
            if d.get(k, -1) < v:
                d[k] = v

        for k in reads:
            if k in self.last_w:
                put(self.last_w[k])
        for k in writes:
            if k in self.last_w:
                put(self.last_w[k])
            for kk, vv in self.readers.get(k, {}).items():
                put((kk, vv))
        return d

    def _commit(self, tok, reads, writes):
        k, v = tok
        for b in reads:
            r = self.readers.setdefault(b, {})
            if r.get(k, -1) < v:
                r[k] = v
        for b in writes:
            self.last_w[b] = tok
            self.readers[b] = {}

    def add(self, eng, fn, r=(), w=()):
        pr = [k for k in r if isinstance(k, tuple) and k and k[0] == "ps"]
        if pr:
            r = [k for k in r if k not in pr]
            w = list(w) + pr
        deps = self._deps(r, w)
        if eng == "pe":
            deps.pop(("e", "pe"), None)
        idx = len(self.ops[eng])
        self.ops[eng].append(dict(fn=fn, deps=deps, epoch=self.epoch, dma=None, tag=self.tag))
        self._commit((("e", eng), idx), r, w)

    def dma(self, q, fn, r=(), w=(), key=None, first=True):
        deps = self._deps(r, w)
        if not first:
            deps.pop(("d", key), None)
        n = self.dma_cnt.get(key, 0) + 1
        self.dma_cnt[key] = n
        self.ops[q].append(dict(fn=fn, deps=deps, epoch=self.epoch, dma=key, tag=self.tag))
        self._commit((("d", key), n * 16), r, w)

    def n_epochs(self):
        return self.epoch + 1

    def emit(self, eng, handle, eng_sems, dma_sems, sig, rank):
        waited = {}
        for idx, op in enumerate(self.ops[eng]):
            for (kind, x), v in op["deps"].items():
                if kind == "e":
                    ep, val = rank[(x, v)]
                    sem = eng_sems[x][ep]
                else:
                    sem, val = dma_sems[x], v
                key = id(sem)
                if waited.get(key, 0) < val:
                    handle.wait_ge(sem, val)
                    waited[key] = val
            ins = op["fn"](handle)
            if op["dma"] is not None:
                ins.then_inc(dma_sems[op["dma"]], 16)
            elif idx in sig[eng]:
                ins.then_inc(eng_sems[eng][op["epoch"]], 1)

    def analyse(self):
        sig = {e: set() for e in self.ENGS}
        for e in self.ENGS:
            for op in self.ops[e]:
                for (kind, x), v in op["deps"].items():
                    if kind == "e":
                        sig[x].add(v)
        rank = {}
        for e in self.ENGS:
            cnt = {}
            for idx, op in enumerate(self.ops[e]):
                if idx in sig[e]:
                    ep = op["epoch"]
                    cnt[ep] = cnt.get(ep, 0) + 1
                    rank[(e, idx)] = (ep, cnt[ep])
        return sig, rank


def build_program(layers, debug=False, nch=NCH, stop=9):
    nc = bass.Bass("TRN2", target_bir_lowering=False)
    NL = len(layers)
    L2 = 2

    def din(name, shape, dt=F32):
        return nc.dram_tensor(name, list(shape), dt, kind="ExternalInput").ap()

    x_d = din("x", [SEQ, D])
    out_d = nc.dram_tensor("out", [SEQ, D], F32, kind="ExternalOutput").ap()
    PLAN = req_plan()
    NREQ = len(PLAN)
    wimg_d = din("wimg", [L2, NREQ, 128, WSLOT])
    wsT_d = din("wsT", [L2, 128, 4, 128])
    bs_d = din("bs", [L2, 512])
    vecs_d = din("vecs", [L2, 128, NV])
    gpre_d = din("gpre", [L2, D])
    gpost_d = din("gpost", [L2, D])
    cos_d = din("cos2", [64, SEQ])
    sin_d = din("sin2", [64, SEQ])
    ident_d = din("ident", [128, 128])

    S = Sched()
    es = ExitStack()

    def sb(name, shape, dt):
        return es.enter_context(nc.sbuf_tensor(name, list(shape), dt))

    with es:
        ps = es.enter_context(nc.psum_tensor("ps", [128, 8, 512], F32))
        x_sb = sb("x_sb", [128, TPC, D], F32)
        ckv_c = [sb(f"ckv{i}", [128, 2, SEQ], BF16) for i in range(NL)]
        kr_c = [sb(f"kr{i}", [128, SEQ], BF16) for i in range(NL)]
        hist = [sb(f"hist{i}", [128, 10, 4], F32) for i in range(NL)]
        hst = [sb(f"hst{i}", [128, 10], F32) for i in range(NL)]
        vecs = [sb(f"vecs{i}", [128, NV], F32) for i in range(NL)]
        nca = [sb(f"nca{i}", [128, 40], F32) for i in range(NL)]
        Bt = [sb(f"Bt{i}", [128, 8, 128], F32) for i in range(NL)]
        wsT_bf = [sb(f"wsTb{i}", [128, 4, 128], BF16) for i in range(NL)]
        ident = sb("ident_sb", [128, 128], BF16)
        ones_f = sb("ones_f", [128, 128], F32)
        ones_b = sb("ones_b", [128, 128], BF16)
        gbc = sb("gbc", [128, D], F32)
        cs_sb = sb("cs_sb", [64, 2, NT], F32)
        ring = [sb(f"ring{i}", [128, WSLOT], BF16) for i in range(NSLOT)]
        hT = sb("hT", [128, 8, NT], BF16)
        T = sb("T", [128, NTB, NT], F32)
        B16 = sb("B16", [128, NB16, NT], BF16)
        htok = sb("htok", [128, 4, D], BF16)
        qbuf = sb("qbuf", [128, 2, 2, NT], BF16)
        small = sb("small", [128, 64], F32)
        LN_QUARTER = sb("lnq", [128, 1], F32)
        bnst = sb("bnst", [128, 2, 6], F32)
        cqn = sb("cqn", [128, 3, NT], BF16)
        k_h = sb("k_h", [128, SEQ], BF16)
        v_h = sb("v_h", [128, 16, 128], BF16)
        y_a = sb("y_a", [128, 8, NT], BF16)
        y_b = sb("y_b", [128, 8, NT], BF16)
        y_c = sb("y_c", [128, 10, NT], BF16)
        xcf = sb("xcf", [128, 5, NT], F32)
        xcb = sb("xcb", [128, 5, NT], BF16)
        raw = sb("raw", [128, 2, NT + 4], F32)

        mrg = htok[:].rearrange("p t (a n) -> p (t a) n", a=2)
        st = dict(ps=0, t=0, b=0, ring=0, sm=0, pslim=8, tlim=NTB)

        def next_ps(n=1):
            b = st["ps"]
            if n == 2 and b % 2 == 1:
                b += 1
            lim = st["pslim"]
            if b + n > lim:
                b = 0
            st["ps"] = (b + n) % lim
            return b

        def psk(b):
            return ("ps", b)

        def next_t(n=1):
            b = st["t"]
            lim = st["tlim"]
            if b + n > lim:
                b = 0
            st["t"] = (b + n) % lim
            return b

        def tk(b):
            return ("T", b)

        def next_b():
            b = st["b"]
            st["b"] = (b + 1) % NB16
            return b

        def bk(b):
            return ("B", b)

        def next_sm(n=1):
            b = st["sm"]
            if b + n > 64:
                b = 0
            st["sm"] = (b + n) % 64
            return b

        wreqs = []
        for c_ in range(nch):
            for l_ in layers:
                for j_, (name_, parts_) in enumerate(PLAN):
                    wreqs.append((name_, parts_, l_, j_))
        wst = dict(cur=0, issued=0, views={})

        def wissue(i):
            name, parts, l_, j_ = wreqs[i]
            s = i % NSLOT
            key = ("w", s)
            views = []
            for off, shp, _spec in parts:
                n = int(np.prod(shp))
                v = ring[s][:, off:off + n]
                if len(shp) == 2:
                    v = v.rearrange("p (a b) -> p a b", a=shp[0], b=shp[1])
                views.append(v)
            used = req_used(parts)
            S.dma("pool", (lambda e, s=s, l_=l_, j_=j_, used=used: e.dma_start(
                out=ring[s][:, 0:used], in_=wimg_d[l_, j_, :, 0:used], max_dma_last_dim=8192)),
                r=(), w=(key,), key=key)
            wst["views"][i] = (key, views)

        def wnext(name):
            i = wst["cur"]
            assert wreqs[i][0] == name, (wreqs[i][0], name)
            while wst["issued"] <= min(i + LOOKAHEAD, len(wreqs) - 1):
                wissue(wst["issued"])
                wst["issued"] += 1
            wst["cur"] = i + 1
            return wst["views"].pop(i)

        def mm(out, lhsT, rhs, start, stop, r, w):
            S.add("pe", (lambda e: e.matmul(out, lhsT, rhs, start=start, stop=stop)), r=r, w=w)

        def act(out, in_, func, r, w, bias=None, scale=None, accum=None):
            kw = {}
            if bias is not None:
                kw["bias"] = bias
            if scale is not None:
                kw["scale"] = scale
            if accum is not None:
                kw["accum_out"] = accum
            S.add("act", (lambda e: e.activation(out=out, in_=in_, func=func, **kw)), r=r, w=w)

        def sigmoid3(dst, src, r_src, w_dst, scale=-1.0, bias=None):
            act(dst, src, AF.Exp, r=r_src, w=w_dst, scale=scale, bias=bias)
            act(dst, dst, AF.Ln, r=w_dst, w=w_dst, bias=1.0)
            act(dst, dst, AF.Exp, r=w_dst, w=w_dst, scale=-1.0)

        def tanh_half(dst, src, r_src, w_dst, bias=None):
            act(dst, src, AF.Tanh, r=r_src, w=w_dst, scale=0.5, bias=bias)

        def dve(fn, r, w):
            S.add("dve", fn, r=r, w=w)

        def tt(out, in0, in1, op, r, w, eng="dve"):
            S.add(eng, (lambda e: e.tensor_tensor(out=out, in0=in0, in1=in1, op=op)), r=r, w=w)

        def stt(out, in0, scalar, in1, op0, op1, r, w):
            S.add("dve", (lambda e: e.scalar_tensor_tensor(out=out, in0=in0, scalar=scalar, in1=in1,
                                                           op0=op0, op1=op1)), r=r, w=w)

        def ts(out, in0, s1, s2, op0, op1, r, w, eng="dve"):
            if s2 is None:
                S.add(eng, (lambda e: e.tensor_scalar(out=out, in0=in0, scalar1=s1, scalar2=None, op0=op0)),
                      r=r, w=w)
            else:
                S.add(eng, (lambda e: e.tensor_scalar(out=out, in0=in0, scalar1=s1, scalar2=s2, op0=op0,
                                                      op1=op1)), r=r, w=w)

        def cp(out, in_, r, w, eng):
            if eng == "act":
                S.add("act", (lambda e: e.activation(out=out, in_=in_, func=AF.Copy)), r=r, w=w)
            else:
                S.add(eng, (lambda e: e.tensor_copy(out=out, in_=in_)), r=r, w=w)

        def rsqrt_small(dst, src, r, w, scale, eps=EPS):
            act(dst, src, AF.Ln, r=r, w=w, bias=eps, scale=scale)
            act(dst, dst, AF.Exp, r=w, w=w, scale=-0.5)

        wsT_f = T[:, 0, :].rearrange("p (g i) -> p g i", g=4)
        bs_bc = T[:, 1, :].rearrange("p (g i) -> p g i", g=4)
        x_view = x_d.rearrange("(t p) d -> p t d", p=128)
        for t_ in range(TPC):
            S.dma("sp", (lambda e, t_=t_: e.dma_start(out=x_sb[:, t_, :], in_=x_view[:, t_, :])),
                  w=(("x", t_),), key=("xl0", t_))
        S.dma("sp", (lambda e: e.dma_start(out=gbc[:], in_=gpre_d[layers[0]].partition_broadcast(128))),
              w=("gbc",), key="gbc")
        S.dma("sp", (lambda e: e.dma_start(out=cs_sb[:, 0, :], in_=cos_d[:, 0:NT])), w=("cs",), key="cs")
        S.dma("sp", (lambda e: e.dma_start(out=cs_sb[:, 1, :], in_=sin_d[:, 0:NT])), w=("cs",), key="cs")
        S.dma("pool", lambda e: e.dma_start(out=ident[:], in_=ident_d[:, :]), w=("ident",), key="c0")
        S.add("dve", lambda e: e.memset(ones_f[:], 1.0), w=("ones_f",))
        S.add("dve", lambda e: e.memset(ones_b[:], 1.0), w=("ones_b",))
        S.add("dve", lambda e: e.memset(LN_QUARTER[:], -1.3862943611198906), w=("lnq",))
        for qp_ in range(2):
            S.add("dve", (lambda e, qp_=qp_: e.memset(qbuf[64:128, qp_, 1, :], 0.0)), w=(("qr", qp_),))
        for li_ in range(NL):
            S.add("dve", (lambda e, li_=li_: e.memset(kr_c[li_][64:128, :], 0.0)), w=(("kr", li_),))
        for li, l in enumerate(layers):
            S.add("dve", (lambda e, li=li: e.memset(hist[li][:], 0.0)), w=(("hist", li),))
            S.add("dve", (lambda e, li=li: e.memset(hst[li][:], 0.0)), w=(("hst", li),))
            S.dma("sp", (lambda e, li=li, l=l: e.dma_start(out=vecs[li][:], in_=vecs_d[l])), w=(("vecs", li),),
                  key=("c1", li))
            S.dma("sp", (lambda e, l=l: e.dma_start(out=wsT_f, in_=wsT_d[l])), w=(tk(0),), key=("c2", li))
            S.dma("sp", (lambda e, l=l: e.dma_start(
                out=T[:, 1, :], in_=bs_d[l].partition_broadcast(128))),
                w=(tk(1),), key=("c3", li))
            S.add("dve", lambda e: e.memset(wsT_f[64:128, :, 0:64], 0.0), r=(), w=(tk(0),))
            cp(wsT_bf[li][:], wsT_f, r=(tk(0),), w=(("wsT_bf", li),), eng="dve")
            b = next_ps()
            cp(B16[:, 0, :], T[:, 0, :], r=(tk(0),), w=(bk(0),), eng="dve")
            tt(B16[:, 1, :], T[:, 0, :], B16[:, 0, :], ALU.subtract, r=(tk(0), bk(0)), w=(bk(1),))
            mm(ps[:, b, :], ones_b[:], B16[:, 0, :], True, False, r=("ones_b", bk(0)), w=(psk(b),))
            mm(ps[:, b, :], ones_b[:], B16[:, 1, :], False, True, r=("ones_b", bk(1)), w=(psk(b),))
            for cc in range(8):
                g = cc // 2
                stt(Bt[li][:, cc, :], ps[:, b, g * 128:(g + 1) * 128], vecs[li][:, V_LNB + cc:V_LNB + cc + 1],
                    bs_bc[:, g, :], ALU.mult, ALU.add, r=(psk(b), ("vecs", li), tk(1)), w=(("Bt", li),))
            lam = vecs[li][:, V_LAM:V_LAM + 10]
            act(nca[li][:, 0:10], lam, AF.Exp, r=(("vecs", li),), w=(("nca", li),), scale=-1.0)
            act(nca[li][:, 0:10], nca[li][:, 0:10], AF.Ln, r=(("nca", li),), w=(("nca", li),), bias=1.0)
            ts(nca[li][:, 10:20], nca[li][:, 0:10], -8.0, None, ALU.mult, None, r=(("nca", li),),
               w=(("nca", li),))
            ts(nca[li][:, 0:10], nca[li][:, 0:10], -4.0, None, ALU.mult, None, r=(("nca", li),), w=(("nca", li),))
            ts(nca[li][:, 20:30], vecs[li][:, V_BA:V_BA + 10], 0.5, None, ALU.mult, None,
               r=(("vecs", li),), w=(("nca", li),))
            ts(nca[li][:, 30:40], vecs[li][:, V_BX:V_BX + 10], 0.5, None, ALU.mult, None,
               r=(("vecs", li),), w=(("nca", li),))

        x_view = x_d.rearrange("(t p) d -> p t d", p=128)
        out_view = out_d.rearrange("(t p) d -> p t d", p=128)

        for c in range(nch):
            c0 = c * NT
            S.epoch += 1
            if c > 0:
                S.dma("sp", (lambda e, c0=c0: e.dma_start(out=cs_sb[:, 0, :], in_=cos_d[:, c0:c0 + NT])),
                      w=("cs",), key="cs")
                S.dma("sp", (lambda e, c0=c0: e.dma_start(out=cs_sb[:, 1, :], in_=sin_d[:, c0:c0 + NT])),
                      w=("cs",), key="cs")
            for li, l in enumerate(layers):
                if li > 0:
                    S.epoch += 1
                VK = ("vecs", li)
                vv = vecs[li]
                st["tlim"] = NTB
                S.tag = 'S0.c%d.l%d' % (c, li)
                sc = next_sm(TPC)
                SMK = tuple(("sm", sc + q_) for q_ in range(TPC))
                for t_ in range(TPC):
                    act(htok[:, t_, :], x_sb[:, t_, :], AF.Square, r=(("x", t_),), w=(("htok", t_), SMK[t_]),
                        accum=small[:, sc + t_:sc + t_ + 1])
                rsqrt_small(small[:, sc:sc + TPC], small[:, sc:sc + TPC], r=SMK, w=SMK, scale=1.0 / D)
                for t_ in range(TPC):
                    xt = x_sb[:, t_, :]
                    ssq = small[:, sc + t_:sc + t_ + 1]
                    stt(htok[:, t_, :], xt, ssq, gbc[:], ALU.mult, ALU.mult, r=(("x", t_), SMK[t_], "gbc"),
                        w=(("htok", t_),))
                    b = next_ps()
                    psb = ps[:, b, :].bitcast(BF16)
                    for kc in range(8):
                        S.add("pe", (lambda e, t_=t_, kc=kc, psb=psb: e.transpose(
                            psb[:, kc * 128:(kc + 1) * 128], htok[:, t_, kc * 128:(kc + 1) * 128], ident[:])),
                            r=(("htok", t_), "ident"), w=(psk(b),))
                    cp(hT[:, :, t_ * 128:(t_ + 1) * 128], psb.rearrange("p (k t) -> p k t", k=8),
                       r=(psk(b),), w=("hT",), eng="act")

                if not (c == nch - 1 and li == NL - 1):
                    l_next = layers[(li + 1) % NL]
                    S.dma("sp", (lambda e, l_next=l_next: e.dma_start(
                        out=gbc[:], in_=gpre_d[l_next].partition_broadcast(128))), w=("gbc",), key="gbc")
                if stop <= 0:
                    continue
                S.tag = 'S1.c%d.l%d' % (c, li)
                wk1, (w_cq,) = wnext("cq")
                wk2, (w_ckv, w_krs) = wnext("ckv")

                def ln_proj(wkey, wv, nun):
                    tb = [next_t() for _ in range(nun)]
                    tq = []
                    for m in range(nun):
                        b = next_ps()
                        for kc in range(8):
                            mm(ps[:, b, :], wv[:, kc, m * 128:(m + 1) * 128], hT[:, kc, :],
                               kc == 0, kc == 7, r=(wkey, "hT"), w=(psk(b),))
                        q_ = next_t()
                        tq.append(q_)
                        act(T[:, q_, :], ps[:, b, :], AF.Square, r=(psk(b),), w=(tk(q_),))
                        cp(T[:, tb[m], :], ps[:, b, :], r=(psk(b),), w=(tk(tb[m]),), eng="dve")
                    return tb, tq

                def ln_hilo(tq):
                    hl = []
                    for q_ in tq:
                        bh, bl = next_b(), next_b()
                        cp(B16[:, bh, :], T[:, q_, :], r=(tk(q_),), w=(bk(bh),), eng="dve")
                        tt(B16[:, bl, :], T[:, q_, :], B16[:, bh, :], ALU.subtract, r=(tk(q_), bk(bh)),
                           w=(bk(bl),))
                        hl.append((bh, bl))
                    return hl

                def ln_ones(hl):
                    bss = next_ps()
                    n_ = len(hl)
                    for m, (bh, bl) in enumerate(hl):
                        mm(ps[:, bss, :], ones_b[:], B16[:, bh, :], m == 0, False, r=("ones_b", bk(bh)),
                           w=(psk(bss),))
                        mm(ps[:, bss, :], ones_b[:], B16[:, bl, :], False, m == n_ - 1, r=("ones_b", bk(bl)),
                           w=(psk(bss),))
                    return bss

                def ln_fin(bss, tb, tq, gcol, dst_fn, dst_key):
                    nun = len(tb)
                    tr = tq[0]
                    act(T[:, tr, :], ps[:, bss, :], AF.Ln, r=(psk(bss),), w=(tk(tr),), bias=EPS,
                        scale=1.0 / (128 * nun))
                    act(T[:, tr, :], T[:, tr, :], AF.Exp, r=(tk(tr),), w=(tk(tr),), scale=-0.5)
                    for m in range(nun):
                        stt(dst_fn(m), T[:, tb[m], :], vv[:, gcol + m:gcol + m + 1], T[:, tr, :], ALU.mult,
                            ALU.mult, r=(tk(tb[m]), tk(tr), VK), w=(dst_key,))

                st["t"] = 0
                tb_q, tq_q = ln_proj(wk1, w_cq, 3)
                hl_q = ln_hilo(tq_q)
                tb_k, tq_k = ln_proj(wk2, w_ckv, 2)
                bss_q = ln_ones(hl_q)
                hl_k = ln_hilo(tq_k)
                bA = next_ps()
                bB = next_ps()
                for kc in range(8):
                    mm(ps[0:64, bA, :], w_ckv[:, kc, 256:320], hT[:, kc, :], kc == 0, kc == 7, r=(wk2, "hT"),
                       w=(psk(bA),))
                for kc in range(8):
                    mm(ps[0:64, bB, :], w_krs[:, kc, :], hT[:, kc, :], kc == 0, kc == 7, r=(wk2, "hT"),
                       w=(psk(bB),))
                bss_k = ln_ones(hl_k)
                ln_fin(bss_q, tb_q, tq_q, V_QG, lambda m: cqn[:, m, :], "cqn")
                ln_fin(bss_k, tb_k, tq_k, V_KVG, lambda m: ckv_c[li][:, m, c0:c0 + NT], ("ckv", li))
                t1 = next_t()
                t2 = next_t()
                tt(T[0:64, t1, :], ps[0:64, bA, :], cs_sb[:, 0, :], ALU.mult, r=(psk(bA), "cs"), w=(tk(t1),))
                tt(T[0:64, t2, :], ps[0:64, bB, :], cs_sb[:, 1, :], ALU.mult, r=(psk(bB), "cs"), w=(tk(t2),))
                tt(kr_c[li][0:64, c0:c0 + NT], T[0:64, t1, :], T[0:64, t2, :], ALU.add, r=(tk(t1), tk(t2)),
                   w=(("kr", li),))

                if stop <= 1:
                    continue
                S.tag = 'Av.c%d.l%d' % (c, li)
                wv_k = []
                wv_v = []
                for n in range(2):
                    k_, (v_,) = wnext("wv%d" % n)
                    wv_k.append(k_)
                    wv_v.append(v_)
                for t_ in range(TPC):
                    b = next_ps(2)
                    for n in range(2):
                        for kc in range(8):
                            mm(ps[:, b + n, :], hT[:, kc, t_ * 128:(t_ + 1) * 128], wv_v[n][:, kc, :], kc == 0,
                               kc == 7, r=(wv_k[n], "hT"), w=(psk(b + n),))
                        S.add("dve", (lambda e, n=n, b=b: e.bn_stats(out=bnst[:, n, :], in_=ps[:, b + n, :])),
                              r=(psk(b + n),), w=("bnst",))
                    sc = next_sm(2)
                    S.add("dve", (lambda e, sc=sc: e.bn_aggr(out=small[:, sc:sc + 2],
                                                             in_=bnst[:].rearrange("p a b -> p (a b)"))),
                          r=("bnst",), w=(("sm", sc), ("sm", sc + 1)))
                    rsqrt_small(small[:, sc + 1:sc + 2], small[:, sc + 1:sc + 2], r=(("sm", sc + 1),),
                                w=(("sm", sc + 1),), scale=1.0)
                    ts(htok[:, t_, :].rearrange("p (n f) -> p n f", n=2), ps[:, b:b + 2, :], small[:, sc:sc + 1],
                       small[:, sc + 1:sc + 2], ALU.subtract, ALU.mult,
                       r=(psk(b), psk(b + 1), ("sm", sc), ("sm", sc + 1)), w=(("htok", t_),))
                S.tag = 'S3.c%d.l%d' % (c, li)
                nk = TPC * (c + 1)
                st["pslim"] = 4
                st["ps"] = st["ps"] % 4

                def head_finalize(h_, tz_, bo_, bs_):
                    tr = next_t()
                    S.add("dve", (lambda e, tr=tr, bs_=bs_: e.reciprocal(out=T[:, tr, :], in_=ps[:, bs_, :])),
                          r=(psk(bs_),), w=(tk(tr),))
                    stt(T[:, tr, :], ps[:, bo_, :], 0.5, T[:, tr, :], ALU.mult, ALU.mult, r=(psk(bo_), tk(tr)),
                        w=(tk(tr),))
                    tt(y_b[:, h_, :], T[:, tr, :], T[:, tz_, :], ALU.mult, r=(tk(tr), tk(tz_)), w=("y_b",),
                       eng="dve")

                pending = None
                for h in range(H):
                    q0c = h * 192
                    bo, bsum = (4, 5) if h % 2 == 0 else (6, 7)
                    wkh, (wq, wqs, wkv, wzb) = wnext("h%d" % h)
                    b = next_ps()
                    for kc in range(3):
                        mm(ps[:, b, :], wq[:, kc, 0:128], cqn[:, kc, :], kc == 0, kc == 2, r=(wkh, "cqn"),
                           w=(psk(b),))
                    qp = h % 2
                    QN = ("qn", qp)
                    QR = ("qr", qp)
                    cp(qbuf[:, qp, 0, :], ps[:, b, :], r=(psk(b),), w=(QN,), eng="act")
                    bA = next_ps()
                    bB = next_ps()
                    for kc in range(3):
                        mm(ps[0:64, bA, :], wq[:, kc, 128:192], cqn[:, kc, :], kc == 0, kc == 2, r=(wkh, "cqn"),
                           w=(psk(bA),))
                    for kc in range(3):
                        mm(ps[0:64, bB, :], wqs[:, kc, :], cqn[:, kc, :], kc == 0, kc == 2, r=(wkh, "cqn"),
                           w=(psk(bB),))
                    t1 = next_t()
                    t2 = next_t()
                    tt(T[0:64, t1, :], ps[0:64, bA, :], cs_sb[:, 0, :], ALU.mult, r=(psk(bA), "cs"), w=(tk(t1),))
                    tt(T[0:64, t2, :], ps[0:64, bB, :], cs_sb[:, 1, :], ALU.mult, r=(psk(bB), "cs"), w=(tk(t2),))
                    tt(qbuf[0:64, qp, 1, :], T[0:64, t1, :], T[0:64, t2, :], ALU.add, r=(tk(t1), tk(t2)),
                       w=(QR,))
                    for kb in range(c + 1):
                        b = next_ps()
                        for kc in range(2):
                            mm(ps[:, b, :], wkv[:, kc, 0:128], ckv_c[li][:, kc, kb * NT:(kb + 1) * NT], kc == 0,
                               kc == 1, r=(wkh, ("ckv", li)), w=(psk(b),))
                        cp(k_h[:, kb * NT:(kb + 1) * NT], ps[:, b, :], r=(psk(b),), w=("k_h",),
                           eng=("act" if kb % 2 else "dve"))
                        b = next_ps()
                        for j in range(4):
                            for kc in range(2):
                                mm(ps[:, b, j * 128:(j + 1) * 128],
                                   ckv_c[li][:, kc, (4 * kb + j) * 128:(4 * kb + j + 1) * 128],
                                   wkv[:, kc, 128:256], kc == 0, kc == 1, r=(wkh, ("ckv", li)), w=(psk(b),))
                        cp(v_h[:, 4 * kb:4 * kb + 4, :], ps[:, b, :].rearrange("p (j d) -> p j d", j=4),
                           r=(psk(b),), w=("v_h",), eng=("dve" if kb % 2 else "act"))
                    bz = next_ps()
                    for kc in range(8):
                        mm(ps[:, bz, :], wzb[:, kc, :], hT[:, kc, :], kc == 0, kc == 7, r=(wkh, "hT"), w=(psk(bz),))
                    tz = next_t()
                    tanh_half(T[:, tz, :], ps[:, bz, :], r_src=(psk(bz),), w_dst=(tk(tz),))
                    stt(T[:, tz, :], T[:, tz, :], 1.0, ps[:, bz, :], ALU.add, ALU.mult, r=(psk(bz), tk(tz)),
                        w=(tk(tz),))
                    if pending is not None:
                        head_finalize(*pending)
                    LOOK = 3
                    sbank = {}

                    def scores(kt):
                        j = kt - TPC * c
                        q0 = 128 * max(j, 0)
                        b = next_ps()
                        sbank[kt] = b
                        mm(ps[:, b, q0:NT], k_h[:, kt * 128:(kt + 1) * 128], qbuf[:, qp, 0, q0:NT], True, False,
                           r=("k_h", QN), w=(psk(b),))
                        mm(ps[:, b, q0:NT], kr_c[li][:, kt * 128:(kt + 1) * 128], qbuf[:, qp, 1, q0:NT], False,
                           True, r=(("kr", li), QR), w=(psk(b),))

                    for kt in range(min(LOOK, nk)):
                        scores(kt)
                    for kt in range(nk):
                        if kt + LOOK < nk:
                            scores(kt + LOOK)
                        j = kt - TPC * c
                        q0 = 128 * max(j, 0)
                        b = sbank.pop(kt)
                        pt = next_b()
                        if j < 0:
                            act(B16[:, pt, :], ps[:, b, :], AF.Exp, r=(psk(b),), w=(bk(pt),), scale=SCALE)
                        else:
                            act(B16[:, pt, q0 + 64:NT], ps[:, b, q0 + 64:NT], AF.Exp, r=(psk(b),), w=(bk(pt),),
                                scale=SCALE)
                            act(B16[0:64, pt, q0:q0 + 64], ps[0:64, b, q0:q0 + 64], AF.Exp, r=(psk(b),),
                                w=(bk(pt),), scale=SCALE)
                            S.add("dve", (lambda e, pt=pt, q0=q0: e.memset(B16[64:128, pt, q0:q0 + 64], 0.0)),
                                  r=(), w=(bk(pt),))
                        mm(ps[:, bo, q0:NT], v_h[:, kt, :], B16[:, pt, q0:NT], kt == 0, kt == nk - 1,
                           r=("v_h", bk(pt)), w=(psk(bo),))
                        mm(ps[:, bsum, q0:NT], ones_b[:], B16[:, pt, q0:NT], kt == 0, kt == nk - 1,
                           r=("ones_b", bk(pt)), w=(psk(bsum),))
                    pending = (h, tz, bo, bsum)
                head_finalize(*pending)

                if stop <= 2:
                    continue
                S.tag = 'A.c%d.l%d' % (c, li)
                st["pslim"] = 8
                wu = {}
                for cc in range(8):
                    if cc % 4 == 0:
                        ku, (vu,) = wnext("u%d" % (cc // 4))
                        kz, (vz,) = wnext("z%d" % (cc // 4))
                    g = cc // 2
                    bs_ = next_ps()
                    for t_ in range(TPC):
                        mm(ps[:, bs_, t_ * 128:(t_ + 1) * 128], htok[:, t_, cc * 128:(cc + 1) * 128],
                           wsT_bf[li][:, g, :], True, True, r=(("htok", t_), ("wsT_bf", li)), w=(psk(bs_),))
                    bu = next_ps()
                    for kc in range(8):
                        mm(ps[:, bu, :], vu[:, kc, (cc % 4) * 128:(cc % 4 + 1) * 128], hT[:, kc, :], kc == 0, kc == 7,
                           r=(ku, "hT"), w=(psk(bu),))
                    bz = next_ps()
                    for kc in range(8):
                        mm(ps[:, bz, :], vz[:, kc, (cc % 4) * 128:(cc % 4 + 1) * 128], hT[:, kc, :], kc == 0, kc == 7,
                           r=(kz, "hT"), w=(psk(bz),))
                    ta = next_t()
                    stt(T[:, ta, :].rearrange("p (t i) -> p t i", t=4),
                        ps[:, bs_, :].rearrange("p (t i) -> p t i", t=4),
                        vv[:, V_LNG + cc:V_LNG + cc + 1],
                        Bt[li][:, cc, :].unsqueeze(1).broadcast_to([128, 4, 128]),
                        ALU.mult, ALU.add, r=(psk(bs_), VK, ("Bt", li)), w=(tk(ta),))
                    stt(T[:, ta, :], ps[:, bu, :], 0.5, T[:, ta, :], ALU.mult, ALU.mult, r=(psk(bu), tk(ta)),
                        w=(tk(ta),))
                    tz = next_t()
                    tanh_half(T[:, tz, :], ps[:, bz, :], r_src=(psk(bz),), w_dst=(tk(tz),))
                    stt(T[:, tz, :], T[:, tz, :], 1.0, ps[:, bz, :], ALU.add, ALU.mult, r=(psk(bz), tk(tz)),
                        w=(tk(tz),))
                    tt(y_a[:, cc, :], T[:, ta, :], T[:, tz, :], ALU.mult, r=(tk(ta), tk(tz)), w=("y_a",),
                       eng="dve")

                if stop <= 3:
                    continue
                S.tag = 'C.c%d.l%d' % (c, li)
                for hf in range(2):
                    for uu in range(5):
                        u = hf * 5 + uu
                        if uu == 0:
                            kx, (vx,) = wnext("xc%da" % hf)
                        elif uu == 3:
                            kx, (vx,) = wnext("xc%db" % hf)
                        uo = uu if uu < 3 else uu - 3
                        b = next_ps()
                        for kc in range(8):
                            mm(ps[:, b, :], vx[:, kc, uo * 128:(uo + 1) * 128], hT[:, kc, :], kc == 0, kc == 7,
                               r=(kx, "hT"), w=(psk(b),))
                        rb = uu % 2
                        RK = ("raw", rb)
                        cp(raw[:, rb, 4:NT + 4], ps[:, b, :], r=(psk(b),), w=(RK,), eng="act")
                        cp(raw[:, rb, 1:4], hist[li][:, u, 1:4], r=(("hist", li),), w=(RK,), eng="dve")
                        cw = V_CW + u * 4
                        act(xcf[:, uu, :], ps[:, b, :], AF.Identity, r=(psk(b), VK), w=(("xcf", uu),),
                            scale=vv[:, cw + 3:cw + 4], bias=vv[:, V_CB + u:V_CB + u + 1])
                        for k in range(3):
                            stt(xcf[:, uu, :], raw[:, rb, 1 + k:1 + k + NT], vv[:, cw + k:cw + k + 1], xcf[:, uu, :],
                                ALU.mult, ALU.add, r=(RK, VK, ("xcf", uu)), w=(("xcf", uu),))
                        cp(hist[li][:, u, 1:4], raw[:, rb, NT + 1:NT + 4], r=(RK,), w=(("hist", li),), eng="dve")
                        cp(xcb[:, uu, :], xcf[:, uu, :], r=(("xcf", uu),), w=(("xcb", uu),), eng="dve")
                    kg, (vga, vgx) = wnext("g%d" % hf)
                    NK = ("nca", li)
                    n = 5
                    st["t"] = 0
                    tA, tI = next_t(n), next_t(n)
                    t_r = next_t()
                    for uu in range(n):
                        u = hf * 5 + uu
                        prs = [(qi, ki) for qi, (ou, ki) in enumerate(GATE_PAIRS) if ou == uu]
                        br = next_ps()
                        for n_, (qi, ki) in enumerate(prs):
                            mm(ps[:, br, :], vga[:, qi, :], xcb[:, ki, :], n_ == 0, n_ == len(prs) - 1,
                               r=(kg, ("xcb", ki)), w=(psk(br),))
                        bi = next_ps()
                        for n_, (qi, ki) in enumerate(prs):
                            mm(ps[:, bi, :], vgx[:, qi, :], xcb[:, ki, :], n_ == 0, n_ == len(prs) - 1,
                               r=(kg, ("xcb", ki)), w=(psk(bi),))
                        tanh_half(T[:, t_r, :], ps[:, br, :], r_src=(psk(br), NK), w_dst=(tk(t_r),),
                                  bias=nca[li][:, 20 + u:21 + u])
                        tanh_half(T[:, tI + uu, :], ps[:, bi, :], r_src=(psk(bi), NK), w_dst=(tk(tI + uu),),
                                  bias=nca[li][:, 30 + u:31 + u])
                        act(T[:, tA + uu, :], T[:, t_r, :], AF.Exp, r=(tk(t_r), NK), w=(tk(tA + uu),),
                            scale=nca[li][:, u:u + 1], bias=nca[li][:, u:u + 1])
                    bzs = []
                    for uu in range(n):
                        if uu == 0:
                            kzc, (vzc,) = wnext("zc%da" % hf)
                        elif uu == 3:
                            kzc, (vzc,) = wnext("zc%db" % hf)
                        uo = uu if uu < 3 else uu - 3
                        bz = next_ps()
                        bzs.append(bz)
                        for kc in range(8):
                            mm(ps[:, bz, :], vzc[:, kc, uo * 128:(uo + 1) * 128], hT[:, kc, :], kc == 0, kc == 7,
                               r=(kzc, "hT"), w=(psk(bz),))
                    AK = tuple(tk(tA + i_) for i_ in range(n))
                    IK = tuple(tk(tI + i_) for i_ in range(n))
                    XK = tuple(("xcf", uu) for uu in range(n))
                    Av_ = T[:, tA:tA + n, :]
                    Iv = T[:, tI:tI + n, :]
                    Xv = xcf[:, 0:n, :]
                    stt(Xv, Iv, 1.0, Xv, ALU.add, ALU.mult, r=IK + XK, w=XK)
                    act(Iv, Av_, AF.Square, r=AK, w=IK)
                    ts(Iv, Iv, 0.9999999, -1.0, ALU.min, ALU.mult, r=IK, w=IK)
                    act(Iv, Iv, AF.Ln, r=IK, w=IK, bias=1.0)
                    act(Iv, Iv, AF.Exp, r=IK + ("lnq",), w=IK, scale=0.5, bias=LN_QUARTER[:, 0:1])
                    tt(Iv, Iv, Xv, ALU.mult, r=IK + XK, w=IK)
                    for uu in range(n):
                        u = hf * 5 + uu
                        HK = ("hst", li, u)
                        S.add("dve", (lambda e, uu=uu, a_=tA + uu, m_=tI + uu, li=li, u=u:
                                      e.tensor_tensor_scan(out=xcf[:, uu, :], data0=T[:, a_, :], data1=T[:, m_, :],
                                                           initial=hst[li][:, u:u + 1], op0=ALU.mult,
                                                           op1=ALU.add)),
                              r=(tk(tA + uu), tk(tI + uu), HK), w=(("xcf", uu),))
                        cp(hst[li][:, u:u + 1], xcf[:, uu, NT - 1:NT], r=(("xcf", uu),), w=(HK,), eng="dve")
                    for uu in range(n):
                        u = hf * 5 + uu
                        bz = bzs[uu]
                        tz = tA + uu
                        tanh_half(T[:, tz, :], ps[:, bz, :], r_src=(psk(bz),), w_dst=(tk(tz),))
                        stt(T[:, tz, :], T[:, tz, :], 1.0, ps[:, bz, :], ALU.add, ALU.mult, r=(psk(bz), tk(tz)),
                            w=(tk(tz),))
                        tt(y_c[:, u, :], xcf[:, uu, :], T[:, tz, :], ALU.mult, r=(("xcf", uu), tk(tz)),
                           w=("y_c",), eng="pool")

                if stop <= 4:
                    continue
                S.tag = 'M.c%d.l%d' % (c, li)
                st["tlim"] = NTB - 2
                st["t"] = 0
                S.dma("sp", (lambda e, l=l: e.dma_start(out=T[:, NTB - 2:NTB, :].rearrange("p a b -> p (a b)"),
                                                        in_=gpost_d[l].partition_broadcast(128))),
                      w=(tk(NTB - 2), tk(NTB - 1)), key="gpo")
                for dc in range(8):
                    kgt, vg = wnext("G%d" % dc)
                    kpr, (vpa, vpb, vpc) = wnext("P%d" % dc)
                    t_acc = next_t()
                    mixers = [(y_a, "y_a", 8, vpa), (y_b, "y_b", 8, vpb), (y_c, "y_c", 10, vpc)]
                    bgs, t_ss = [], []
                    for m in range(3):
                        bg = next_ps()
                        for kc in range(8):
                            mm(ps[:, bg, :], vg[m][:, kc, :], hT[:, kc, :], kc == 0, kc == 7, r=(kgt, "hT"),
                               w=(psk(bg),))
                        t_s = next_t()
                        tanh_half(T[:, t_s, :], ps[:, bg, :], r_src=(psk(bg),), w_dst=(tk(t_s),))
                        bgs.append(bg)
                        t_ss.append(t_s)
                    MGK = ("htok", dc // 2)
                    for m, (yb, ykey, nkc, wp) in enumerate(mixers):
                        bp = next_ps()
                        for k in range(nkc):
                            mm(ps[:, bp, :], wp[:, k, :], yb[:, k, :], k == 0, k == nkc - 1, r=(kpr, ykey),
                               w=(psk(bp),))
                        t_s = t_ss[m]
                        if m == 0:
                            stt(T[:, t_acc, :], T[:, t_s, :], 1.0, ps[:, bp, :], ALU.add, ALU.mult,
                                r=(psk(bp), tk(t_s)), w=(tk(t_acc),))
                        else:
                            stt(T[:, t_s, :], T[:, t_s, :], 1.0, ps[:, bp, :], ALU.add, ALU.mult,
                                r=(psk(bp), tk(t_s)), w=(tk(t_s),))
                            if m == 1:
                                tt(T[:, t_acc, :], T[:, t_acc, :], T[:, t_s, :], ALU.add, r=(tk(t_acc), tk(t_s)),
                                   w=(tk(t_acc),), eng="dve")
                            else:
                                tt(mrg[:, dc, :], T[:, t_acc, :], T[:, t_s, :], ALU.add, r=(tk(t_acc), tk(t_s)),
                                   w=(MGK,), eng="dve")

                if stop <= 5:
                    continue
                S.tag = 'O.c%d.l%d' % (c, li)
                wo_k = []
                wo_v = []
                for n in range(2):
                    k_, (v_,) = wnext("wo%d" % n)
                    wo_k.append(k_)
                    wo_v.append(v_)
                for t_ in range(TPC):
                    b = next_ps(2)
                    for n in range(2):
                        for kc in range(8):
                            mm(ps[:, b + n, :], mrg[:, kc, t_ * 128:(t_ + 1) * 128], wo_v[n][:, kc, :], kc == 0,
                               kc == 7, r=(wo_k[n],) + tuple(("htok", q_) for q_ in range(4)), w=(psk(b + n),))
                    sc = next_sm()
                    ssq = small[:, sc:sc + 1]
                    act(y_a[:, 2 * t_:2 * t_ + 2, :], ps[:, b:b + 2, :], AF.Square,
                        r=(psk(b), psk(b + 1)), w=("y_a", ("sm", sc)), accum=ssq)
                    rsqrt_small(ssq, ssq, r=(("sm", sc),), w=(("sm", sc),), scale=1.0 / D, eps=4.0 * EPS)
                    t2 = next_t(2)
                    stt(T[:, t2:t2 + 2, :], ps[:, b:b + 2, :], ssq, T[:, NTB - 2:NTB, :],
                        ALU.mult, ALU.mult, r=(psk(b), psk(b + 1), ("sm", sc), tk(NTB - 2), tk(NTB - 1)),
                        w=(tk(t2), tk(t2 + 1)))
                    tt(x_sb[:, t_, :].rearrange("p (n f) -> p n f", n=2),
                       x_sb[:, t_, :].rearrange("p (n f) -> p n f", n=2), T[:, t2:t2 + 2, :], ALU.add,
                       r=(("x", t_), tk(t2), tk(t2 + 1)), w=(("x", t_),), eng="dve")
                    if li == NL - 1:
                        S.dma("sp", (lambda e, c=c, t_=t_: e.dma_start(out=out_view[:, c * TPC + t_, :],
                                                                        in_=x_sb[:, t_, :])),
                              r=(("x", t_),), w=(("outd", t_),), key=("xo", t_))
                        if c + 1 < nch:
                            S.dma("pool", (lambda e, c=c, t_=t_: e.dma_start(out=x_sb[:, t_, :],
                                                                              in_=x_view[:, (c + 1) * TPC + t_, :])),
                                  w=(("x", t_),), key=("xl", t_))

        n_out = {t_: S.dma_cnt.get(("xo", t_), 0) for t_ in range(TPC)}

        sig, rank = S.analyse()
        nep = S.n_epochs()
        eng_sems = {e: [es.enter_context(nc.semaphore(f"s_{e}_{i}")) for i in range(nep)] for e in Sched.ENGS}
        dma_sems = {k: es.enter_context(nc.semaphore(f"d_{i}")) for i, k in enumerate(S.dma_cnt.keys())}
        block = es.enter_context(nc.Block())

        @block.tensor
        def _(e):
            S.emit("pe", e, eng_sems, dma_sems, sig, rank)

        @block.scalar
        def _(e):
            S.emit("act", e, eng_sems, dma_sems, sig, rank)

        @block.vector
        def _(e):
            S.emit("dve", e, eng_sems, dma_sems, sig, rank)

        @block.gpsimd
        def _(e):
            S.emit("pool", e, eng_sems, dma_sems, sig, rank)

        @block.sync
        def _(e):
            S.emit("sp", e, eng_sems, dma_sems, sig, rank)
            for t_ in range(TPC):
                if n_out[t_]:
                    e.wait_ge(dma_sems[("xo", t_)], 16 * n_out[t_])

    nc._sched = S
    return nc


_PROG_CACHE = {}


def _host_layout(inp):
    L = 2
    f32 = np.float32
    def pad_gate(w):
        full = np.zeros((L, 1280, 1280), f32)
        for hb in range(16):
            full[:, hb * 80:(hb + 1) * 80, hb * 80:(hb + 1) * 80] = w[:, hb]
        tiles = np.zeros((L, 26, 128, 128), f32)
        for hf in range(2):
            for qi, (ou, ki) in enumerate(GATE_PAIRS):
                r0 = (hf * 5 + ki) * 128
                c0 = (hf * 5 + ou) * 128
                tiles[:, hf * 13 + qi] = full[:, r0:r0 + 128, c0:c0 + 128]
        return tiles

    vecs = np.zeros((L, 128, NV), f32)

    def chunks(v, n):
        return np.ascontiguousarray(v.reshape(L, n, 128).transpose(0, 2, 1))

    vecs[:, :, V_LNG:V_LNG + 8] = chunks(inp["gm_ln_g"], 8)
    vecs[:, :, V_LNB:V_LNB + 8] = chunks(inp["gm_ln_b"], 8)
    vecs[:, :, V_QG:V_QG + 3] = chunks(inp["mla_q_norm_g"], 3)
    vecs[:, :, V_KVG:V_KVG + 2] = chunks(inp["mla_kv_norm_g"], 2)
    cw = inp["lru_conv_w"].reshape(L, 4, 10, 128).transpose(0, 3, 2, 1)
    vecs[:, :, V_CW:V_CW + 40] = cw.reshape(L, 128, 40)
    vecs[:, :, V_CB:V_CB + 10] = chunks(inp["lru_conv_b"], 10)
    vecs[:, :, V_BA:V_BA + 10] = chunks(inp["lru_b_a"], 10)
    vecs[:, :, V_BX:V_BX + 10] = chunks(inp["lru_b_x"], 10)
    vecs[:, :, V_LAM:V_LAM + 10] = chunks(inp["lru_lambda"], 10)

    pos = np.arange(SEQ, dtype=f32)
    inv_freq = (10000.0 ** (-np.arange(0, 64, 2, dtype=f32) / f32(64))).astype(f32)
    ang = (pos[:, None] * inv_freq[None, :]).astype(f32)
    cos = np.cos(ang).astype(f32).T
    sin = np.sin(ang).astype(f32).T
    arrs = dict(
        w_in=np.asarray(inp["w_in"], f32),
        w_uq=np.asarray(inp["mla_w_uq"], f32),
        w_ukv=np.asarray(inp["mla_w_ukv"], f32),
        w_pa=np.asarray(inp["w_proj_a"], f32),
        w_pb=np.asarray(inp["w_proj_b"], f32),
        w_pc=np.asarray(inp["w_proj_c"], f32),
        w_out=np.asarray(inp["w_out"], f32),
        wa_pad=pad_gate(np.asarray(inp["lru_w_a"], f32)),
        wx_pad=pad_gate(np.asarray(inp["lru_w_x"], f32)),
    )
    shared = dict(
        wimg=build_slot_images(arrs),
        wsT=np.ascontiguousarray(np.asarray(inp["gm_ws"], f32).transpose(0, 3, 1, 2)),
        bs=np.ascontiguousarray(np.asarray(inp["gm_bs"], f32).reshape(L, 512)),
        vecs=vecs,
        gpre=np.ascontiguousarray(inp["pre_norm_g"], f32),
        gpost=np.ascontiguousarray(inp["post_norm_g"], f32),
        cos2=np.ascontiguousarray(np.concatenate([cos, cos], 0)),
        sin2=np.ascontiguousarray(np.concatenate([-sin, sin], 0)),
        ident=np.eye(128, dtype=f32),
    )
    return shared


def _run(layers, x, shared):
    key = tuple(layers)
    if key not in _PROG_CACHE:
        _PROG_CACHE[key] = build_program(list(layers))
    nc = _PROG_CACHE[key]
    in_maps = []
    for b in range(8):
        m = dict(shared)
        m["x"] = np.ascontiguousarray(x[b], np.float32)
        in_maps.append(m)
    res = run_bass_kernel_spmd(nc, in_maps, core_ids=list(range(8)))
    return np.stack([np.asarray(r["out"], np.float32) for r in res.results], 0)


FUSED = True


def kernel(**inputs):
    inp = {k: np.asarray(v) for k, v in inputs.items()}
    shared = _host_layout(inp)
    x = np.asarray(inp["x"], np.float32)
    if FUSED:
        return _run((0, 1), x, shared)
    y = _run((0,), x, shared)
    return _run((1,), y, shared)
```
